# Optimizing a Trainium2 kernel written in Bass

```python
import math
import jax, jax.numpy as jnp
from jax import lax
import numpy as np

D_MODEL = 1024
BATCH = 16
SEQ = 2048
DEPTH = 2

N_MIXERS = 2
EXPAND = 2
E = EXPAND * D_MODEL
A_HEADS = 4
A_DH = E // A_HEADS
A_CONV = 4
QKV_BLOCK = 4
CHUNK = 64
B_CONV = 3
PLE_DIM = 256
N_A = (DEPTH + 1) // 2
N_B = DEPTH // 2
ALPHA = (2.0 * DEPTH) ** 0.25
BETA = (8.0 * DEPTH) ** -0.25
LN_EPS = 1e-5
GN_EPS = 1e-6

kernel_name = "hybrid_mlstm_shortconv_deepnorm"


def _layer_norm(x, g, b):
    xf = x.astype(jnp.float32)
    mu = jnp.mean(xf, axis=-1, keepdims=True)
    var = jnp.mean(jnp.square(xf - mu), axis=-1, keepdims=True)
    y = (xf - mu) * lax.rsqrt(var + LN_EPS) * g.astype(jnp.float32) + b.astype(jnp.float32)
    return y.astype(x.dtype)


def _causal_dwconv(u, w):
    K = w.shape[0]
    return lax.conv_general_dilated(
        u, w[:, None, :].astype(u.dtype), window_strides=(1,), padding=[(K - 1, 0)],
        dimension_numbers=("NWC", "WIO", "NWC"), feature_group_count=u.shape[-1])


def _blockdiag(u, w):
    g, blk, _ = w.shape
    ub = u.reshape(u.shape[:-1] + (g, blk))
    return jnp.einsum("bsgi,gio->bsgo", ub, w.astype(u.dtype)).reshape(u.shape)


def _mlstm_cell(q, k, v, i_pre, f_pre):
    Bsz, S, _ = q.shape
    nc = S // CHUNK

    def heads(t):
        return t.astype(jnp.float32).reshape(Bsz, nc, CHUNK, A_HEADS, A_DH).transpose(1, 0, 3, 2, 4)

    def gates(t):
        return t.astype(jnp.float32).reshape(Bsz, nc, CHUNK, A_HEADS).transpose(1, 0, 3, 2)

    qh = heads(q)
    kh = heads(k) * (A_DH ** -0.5)
    vh = heads(v)
    ig = gates(i_pre)
    lf = jax.nn.log_sigmoid(gates(f_pre))
    causal = jnp.tril(jnp.ones((CHUNK, CHUNK), dtype=bool))

    def body(carry, xs):
        C, n, m = carry
        qc, kc, vc, igc, lfc = xs
        a = jnp.cumsum(lfc, axis=-1)
        Dm = a[..., :, None] - a[..., None, :] + igc[..., None, :]
        Dm = jnp.where(causal, Dm, -jnp.inf)
        g = a + m[..., None]
        mj = jnp.maximum(g, jnp.max(Dm, axis=-1))
        sc = jnp.einsum("bhjd,bhsd->bhjs", qc, kc) * jnp.exp(Dm - mj[..., None])
        inter = jnp.exp(g - mj)
        num = jnp.einsum("bhjs,bhsd->bhjd", sc, vc) \
            + inter[..., None] * jnp.einsum("bhjk,bhvk->bhjv", qc, C)
        den = jnp.sum(sc, axis=-1) + inter * jnp.einsum("bhjk,bhk->bhj", qc, n)
        h = num / jnp.maximum(jnp.abs(den), jnp.exp(-mj))[..., None]
        aL = a[..., -1]
        ws = aL[..., None] - a + igc
        m_new = jnp.maximum(aL + m, jnp.max(ws, axis=-1))
        decay = jnp.exp(ws - m_new[..., None])
        carry_scale = jnp.exp(aL + m - m_new)
        C_new = carry_scale[..., None, None] * C + jnp.einsum("bhsv,bhsk->bhvk", vc * decay[..., None], kc)
        n_new = carry_scale[..., None] * n + jnp.einsum("bhs,bhsk->bhk", decay, kc)
        return (C_new, n_new, m_new), h

    init = (jnp.zeros((Bsz, A_HEADS, A_DH, A_DH), jnp.float32),
            jnp.zeros((Bsz, A_HEADS, A_DH), jnp.float32),
            jnp.zeros((Bsz, A_HEADS), jnp.float32))
    _, hs = lax.scan(body, init, (qh, kh, vh, ig, lf))
    return hs.transpose(1, 0, 3, 2, 4).reshape(Bsz, S, A_HEADS, A_DH)


def _mlstm_branch(x, w_in, conv_w, conv_b, w_q, w_k, w_v, b_i, b_f, gn_w, skip, w_out):
    Bsz, S, _ = x.shape
    proj = x @ w_in.astype(x.dtype)
    xm, o_pre, z, gts = jnp.split(proj, [E, 2 * E, 3 * E], axis=-1)
    i_pre = gts[..., :A_HEADS] + b_i.astype(x.dtype)
    f_pre = gts[..., A_HEADS:] + b_f.astype(x.dtype)
    xc = jax.nn.silu(_causal_dwconv(xm, conv_w) + conv_b.astype(x.dtype))
    q = _blockdiag(xc, w_q)
    k = _blockdiag(xc, w_k)
    v = _blockdiag(xm, w_v)
    h = _mlstm_cell(q, k, v, i_pre, f_pre)
    mu = jnp.mean(h, axis=-1, keepdims=True)
    var = jnp.mean(jnp.square(h - mu), axis=-1, keepdims=True)
    hn = ((h - mu) * lax.rsqrt(var + GN_EPS)).reshape(Bsz, S, E) * gn_w.astype(jnp.float32)
    y = (jax.nn.sigmoid(o_pre.astype(jnp.float32)) * hn
         + skip.astype(jnp.float32) * xc.astype(jnp.float32)) * jax.nn.silu(z.astype(jnp.float32))
    return y.astype(x.dtype) @ w_out.astype(x.dtype)


def _shortconv_branch(x, w_in, conv_w, w_out):
    proj = x @ w_in.astype(x.dtype)
    bg, cg, hx, z = jnp.split(proj, 4, axis=-1)
    u = _causal_dwconv(cg * hx, conv_w)
    y = jax.nn.silu(z) * bg * u
    return y @ w_out.astype(x.dtype)


def setup_inputs(seed: int = 0) -> dict:
    key = jax.random.key(seed)
    ks = jax.random.split(key, 24)
    nrm = jax.random.normal
    f32 = jnp.float32
    return {
        "x": nrm(ks[0], (BATCH, SEQ, D_MODEL), f32),
        "p": nrm(ks[1], (DEPTH, BATCH, SEQ, PLE_DIM), f32),
        "a_w_in": nrm(ks[2], (N_A, D_MODEL, 3 * E + 2 * A_HEADS), f32) * D_MODEL ** -0.5,
        "a_conv_w": nrm(ks[3], (N_A, A_CONV, E), f32) * A_CONV ** -0.5,
        "a_conv_b": 0.01 * nrm(ks[4], (N_A, E), f32),
        "a_w_q": nrm(ks[5], (N_A, E // QKV_BLOCK, QKV_BLOCK, QKV_BLOCK), f32) * QKV_BLOCK ** -0.5,
        "a_w_k": nrm(ks[6], (N_A, E // QKV_BLOCK, QKV_BLOCK, QKV_BLOCK), f32) * QKV_BLOCK ** -0.5,
        "a_w_v": nrm(ks[7], (N_A, E // QKV_BLOCK, QKV_BLOCK, QKV_BLOCK), f32) * QKV_BLOCK ** -0.5,
        "a_b_i": 0.1 * nrm(ks[8], (N_A, A_HEADS), f32),
        "a_b_f": jnp.linspace(3.0, 6.0, A_HEADS, dtype=f32)[None, :] + 0.1 * nrm(ks[9], (N_A, A_HEADS), f32),
        "a_gn_w": 1.0 + 0.02 * nrm(ks[10], (N_A, E), f32),
        "a_skip": 1.0 + 0.02 * nrm(ks[11], (N_A, E), f32),
        "a_w_out": nrm(ks[12], (N_A, E, D_MODEL), f32) * (E ** -0.5) * BETA,
        "b_w_in": nrm(ks[13], (N_B, D_MODEL, 4 * E), f32) * D_MODEL ** -0.5,
        "b_conv_w": nrm(ks[14], (N_B, B_CONV, E), f32) * B_CONV ** -0.5,
        "b_w_out": nrm(ks[15], (N_B, E, D_MODEL), f32) * (E ** -0.5) * BETA,
        "ln_g": 1.0 + 0.02 * nrm(ks[16], (DEPTH, D_MODEL), f32),
        "ln_b": 0.02 * nrm(ks[17], (DEPTH, D_MODEL), f32),
        "ple_w": nrm(ks[18], (DEPTH, PLE_DIM, D_MODEL), f32) * PLE_DIM ** -0.5,
        "ple_gate_w": nrm(ks[19], (DEPTH, D_MODEL, D_MODEL), f32) * D_MODEL ** -0.5,
    }


def reference(x, p, a_w_in, a_conv_w, a_conv_b, a_w_q, a_w_k, a_w_v, a_b_i, a_b_f, a_gn_w,
              a_skip, a_w_out, b_w_in, b_conv_w, b_w_out, ln_g, ln_b, ple_w, ple_gate_w):
    for i in range(DEPTH):
        j = i // N_MIXERS
        if i % N_MIXERS == 0:
            y = _mlstm_branch(x, a_w_in[j], a_conv_w[j], a_conv_b[j], a_w_q[j], a_w_k[j], a_w_v[j],
                              a_b_i[j], a_b_f[j], a_gn_w[j], a_skip[j], a_w_out[j])
        else:
            y = _shortconv_branch(x, b_w_in[j], b_conv_w[j], b_w_out[j])
        x = _layer_norm(ALPHA * x + y, ln_g[i], ln_b[i])
        gate = jax.nn.sigmoid(x @ ple_gate_w[i].astype(x.dtype))
        x = x + gate * (p[i].astype(x.dtype) @ ple_w[i].astype(x.dtype))
    return x
```

```python
import os
import numpy as np
import concourse.bass as bass
import concourse.mybir as mybir
from concourse.bass_utils import run_bass_kernel_spmd

F32 = mybir.dt.float32
BF16 = mybir.dt.bfloat16
AF = mybir.ActivationFunctionType
ALU = mybir.AluOpType

NCORES = 8
D = 1024
E = 2048
H = 4
DH = 512
PLE = 256
SEQ = 2048
BPC = 2
TOK = BPC * SEQ
TT = 512
NCH = TT // 128
TPS = SEQ // TT
NT_FULL = TOK // TT
NDC = D // 128
NEC = E // 128
NA_IN = 3 * E + 2 * H
NB_IN = 4 * E
ALPHA = float((2.0 * 2) ** 0.25)
LN_EPS = 1e-5
GN_EPS = 1e-6
KSC = float(DH ** -0.5)

O_CWA, O_CBA, O_GN, O_SKIP, O_CWB, O_LNG, O_LNB, O_BI, O_BF = 0, 64, 80, 96, 112, 160, 176, 192, 193
O_GNH = 194
NPAR = 210
NSLOT = 4
NOSELF = False
SLOT_BYTES = 8192


class Res:
    __slots__ = ("name", "w", "r", "excl")

    def __init__(self, name):
        self.name = name
        self.w = None
        self.r = {}
        self.excl = isinstance(name, tuple) and name[0] in ("ps", "psb")


class Eng:
    def __init__(self, name, eng, sem, is_pe=False):
        self.name = name
        self.eng = eng
        self.sem = sem
        self.is_pe = is_pe
        self.cnt = 0
        self.seen = {}


class Prog:
    def __init__(self, nt=NT_FULL, debug=()):
        self.nt = nt
        self.debug = set(debug)
        self.dbg_outs = {}
        nc = self.nc = bass.Bass("TRN2", target_bir_lowering=False)
        self.PE = Eng("pe", nc.tensor, nc.alloc_semaphore("s_pe"), True)
        self.ACT = Eng("act", nc.scalar, nc.alloc_semaphore("s_act"))
        self.DVE = Eng("dve", nc.vector, nc.alloc_semaphore("s_dve"))
        self.POOL = Eng("pool", nc.gpsimd, nc.alloc_semaphore("s_pool"))
        self.SP = Eng("sp", nc.sync, nc.alloc_semaphore("s_sp"))
        self.res = {}
        self.dcnt = {}
        self.sb_off = (nc.sbuf_base + 63) // 64 * 64
        self.sb_top = nc.sbuf_top
        self.n_inst = 0

    def R(self, *k):
        r = self.res.get(k)
        if r is None:
            r = self.res[k] = Res(k)
        return r

    def Rs(self, name, n):
        return [self.R(name, i) for i in range(n)]

    def _wait(self, E, reads, writes):
        need = {}

        def upd(tok):
            s, v = tok
            if need.get(s, 0) < v:
                need[s] = v

        for r in reads:
            if r.w is not None:
                upd(r.w)
        for w in writes:
            if w.w is not None:
                upd(w.w)
            for s, v in w.r.items():
                upd((s, v))
        for s, v in need.items():
            if s == E.sem and (E.is_pe or NOSELF):
                continue
            if E.seen.get(s, 0) >= v:
                continue
            E.eng.wait_ge(s, v)
            E.seen[s] = v

    def op(self, E, fn, reads=(), writes=(), inc=True):
        writes = list(writes) + [r for r in reads if r.excl]
        reads = [r for r in reads if not r.excl]
        self._wait(E, reads, writes)
        ins = fn(E.eng)
        self.n_inst += 1
        if inc:
            E.cnt += 1
            ins.then_inc(E.sem, 1)
            v = E.cnt
        else:
            assert E.is_pe
            v = E.cnt + 1
        for r in reads:
            if r.r.get(E.sem, 0) < v:
                r.r[E.sem] = v
        for w in writes:
            w.w = (E.sem, v)
            w.r = {}
        return ins

    def dma(self, Q, out, in_, sem, reads=(), writes=()):
        writes = list(writes) + [r for r in reads if r.excl]
        reads = [r for r in reads if not r.excl]
        self._wait(Q, reads, writes)
        Q.eng.dma_start(out=out, in_=in_).then_inc(sem, 16)
        self.n_inst += 1
        v = self.dcnt[sem] = self.dcnt.get(sem, 0) + 16
        for r in reads:
            r.r[sem] = v
        for w in writes:
            w.w = (sem, v)
            w.r = {}

    def transfer(self, olds, news):
        m = {}
        for o in olds:
            if o.w is not None:
                s, v = o.w
                if m.get(s, 0) < v:
                    m[s] = v
            for s, v in o.r.items():
                if m.get(s, 0) < v:
                    m[s] = v
        for n in news:
            for s, v in m.items():
                if n.r.get(s, 0) < v:
                    n.r[s] = v

    def sb(self, name, shape, dt, at=None):
        nb = int(np.prod(shape[1:])) * (4 if dt == F32 else 2)
        nb = (nb + 63) // 64 * 64
        if at is None:
            off = self.sb_off
            self.sb_off += nb
            assert self.sb_off <= self.sb_top, (name, self.sb_off, self.sb_top)
        else:
            off = at
        t = self.nc.alloc_sbuf_tensor_at(name, list(shape), dt, offset=off)
        return t, off, nb

    def dump(self, name, ap, shape, dt=F32, rd=()):
        if name not in self.debug or name in self.dbg_outs:
            return
        nc = self.nc
        o = nc.dram_tensor("dbg_" + name, list(shape), dt, kind="ExternalOutput").ap()
        self.dbg_outs[name] = (list(shape), dt)
        sem = self.s_dbg
        self.dma(self.SP, o, ap, sem, reads=rd)


def build(nt=NT_FULL, debug=(), stop=None):
    P = Prog(nt, debug)
    nc = P.nc
    PE, ACT, DVE, POOL, SP = P.PE, P.ACT, P.DVE, P.POOL, P.SP
    R = P.R

    x_d = nc.dram_tensor("x", [TOK, D], F32, kind="ExternalInput").ap()
    p_d = nc.dram_tensor("p", [2, TOK, PLE], F32, kind="ExternalInput").ap()
    awin_d = nc.dram_tensor("a_w_in", [D, NA_IN], F32, kind="ExternalInput").ap()
    awout_d = nc.dram_tensor("a_w_out", [E, D], F32, kind="ExternalInput").ap()
    bwin_d = nc.dram_tensor("b_w_in", [D, NB_IN], F32, kind="ExternalInput").ap()
    bwout_d = nc.dram_tensor("b_w_out", [E, D], F32, kind="ExternalInput").ap()
    plew_d = nc.dram_tensor("ple_w", [2 * PLE, D], F32, kind="ExternalInput").ap()
    pleg_d = nc.dram_tensor("ple_gate_w", [2 * D, D], F32, kind="ExternalInput").ap()
    bd_d = nc.dram_tensor("bd", [128, 3 * NEC * 128], F32, kind="ExternalInput").ap()
    par_d = nc.dram_tensor("params", [128, NPAR], F32, kind="ExternalInput").ap()
    out_d = nc.dram_tensor("out", [TOK, D], F32, kind="ExternalOutput").ap()
    awin_b = nc.dram_tensor("awin_b", [D, NA_IN], BF16, kind="Internal").ap()
    awout_b = nc.dram_tensor("awout_b", [E, D], BF16, kind="Internal").ap()
    bwin_b = nc.dram_tensor("bwin_b", [D, NB_IN], BF16, kind="Internal").ap()
    bwout_b = nc.dram_tensor("bwout_b", [E, D], BF16, kind="Internal").ap()
    plew_b = nc.dram_tensor("plew_b", [2 * PLE, D], BF16, kind="Internal").ap()
    pleg_b = nc.dram_tensor("pleg_b", [2 * D, D], BF16, kind="Internal").ap()

    s_const = nc.alloc_semaphore("s_const")
    s_x = nc.alloc_semaphore("s_x")
    s_p = [nc.alloc_semaphore("s_p0"), nc.alloc_semaphore("s_p1")]
    s_o = nc.alloc_semaphore("s_o")
    P.s_dbg = nc.alloc_semaphore("s_dbg")
    s_ring = [nc.alloc_semaphore(f"s_ring{i}") for i in range(NSLOT)]

    xres, _, _ = P.sb("xres", [128, NDC, TT], F32)
    xT, _, _ = P.sb("xT", [128, NDC, TT], BF16)
    ring = []
    for i in range(NSLOT):
        t, _, _ = P.sb(f"ring{i}", [128, SLOT_BYTES // 2], BF16)
        ring.append(t)
    yT, yT_off, _ = P.sb("yT", [128, NEC, TT], BF16)
    x_in, _, _ = P.sb("x_in", [128, NCH, D], F32, at=yT_off)
    head_off = P.sb_off
    xmT, _, _ = P.sb("xmT", [128, 4, TT + 4], BF16)
    xcT, _, _ = P.sb("xcT", [128, 4, TT], BF16)
    qT, _, _ = P.sb("qT", [128, 4, TT], BF16)
    kT, _, _ = P.sb("kT", [128, 4, TT], BF16)
    ktok, ktok_off, _ = P.sb("ktok", [128, NCH, DH], BF16)
    vtok, vtok_off, _ = P.sb("vtok", [128, NCH, DH], BF16)
    hnT, _, _ = P.sb("hnT", [128, 4, TT], BF16)
    sigo, _, _ = P.sb("sigo", [128, 4, TT], BF16)
    siluz, _, _ = P.sb("siluz", [128, 4, TT], BF16)
    hn_tok, _, _ = P.sb("hn_tok", [128, DH], BF16)
    PT, _, _ = P.sb("PT", [128, 128], BF16)
    t1, _, _ = P.sb("t1", [128, TT], F32)
    t2, _, _ = P.sb("t2", [128, TT], F32)
    head_end = P.sb_off
    ostage, _, _ = P.sb("ostage", [128, NCH, D], F32, at=head_off)
    l1_off = head_off + NCH * D * 4
    uin, _, nb_ = P.sb("uin", [128, 4, TT + 4], BF16, at=l1_off)
    l1_off += nb_
    cgs, _, nb_ = P.sb("cgs", [128, TT], F32, at=l1_off)
    l1_off += nb_
    bgs, _, nb_ = P.sb("bgs", [128, TT], BF16, at=l1_off)
    l1_off += nb_
    szs, _, nb_ = P.sb("szs", [128, TT], BF16, at=l1_off)
    l1_off += nb_
    bz, _, nb_ = P.sb("bz", [128, TT], BF16, at=l1_off)
    l1_off += nb_
    assert l1_off <= head_end
    CT, _, _ = P.sb("CT", [128, H, 4, DH], F32)
    CTb, _, _ = P.sb("CTb", [128, H, 4, DH], BF16)
    nst, _, _ = P.sb("nst", [128, H, 4], F32)
    nbb, _, _ = P.sb("nbb", [128, H, 4], BF16)
    bd, _, _ = P.sb("bd_sb", [128, 3, NEC, 128], BF16)
    diag, _, _ = P.sb("diag", [128, 4, 4, 128], BF16)
    g_off = P.sb_off
    gi, _, _ = P.sb("gi", [4, TT], F32)
    gL, _, _ = P.sb("gL", [4, TT], F32)
    gNa, _, _ = P.sb("gNa", [4, TT], F32)
    gB, _, _ = P.sb("gB", [4, TT], F32)
    gG, _, _ = P.sb("gG", [4, TT], F32)
    gu, _, _ = P.sb("gu", [4, TT], F32)
    gfl, _, _ = P.sb("gfl", [4, TT], F32)
    g_end = P.sb_off
    o_ = g_off
    meanb, _, nb_ = P.sb("meanb", [128, TT], F32, at=o_)
    o_ += nb_
    rstdb, _, nb_ = P.sb("rstdb", [128, TT], F32, at=o_)
    o_ += nb_
    nbias, _, nb_ = P.sb("nbias", [128, TT], F32, at=o_)
    o_ += nb_
    sq0, _, nb_ = P.sb("sq0", [128, TT], F32, at=o_)
    o_ += nb_
    sq1, _, nb_ = P.sb("sq1", [128, TT], F32, at=o_)
    o_ += nb_
    gsb0, _, nb_ = P.sb("gsb0", [128, TT], F32, at=o_)
    o_ += nb_
    gsb1, _, nb_ = P.sb("gsb1", [128, TT], F32, at=o_)
    o_ += nb_
    assert o_ <= g_end
    sq = [sq0, sq1]
    gsb = [gsb0, gsb1]
    ident_f, _, _ = P.sb("ident_f", [128, 128], F32)
    ident_b, _, _ = P.sb("ident_b", [128, 128], BF16)
    mask_b, _, _ = P.sb("mask_b", [128, 128], BF16)
    onesD, _, _ = P.sb("onesD", [128, 128], F32)
    ones_f, _, _ = P.sb("ones_f", [128, 128], F32)
    par, _, _ = P.sb("par", [128, NPAR], F32)
    wg, _, _ = P.sb("wg", [128, NDC, 8], BF16)
    pstage0, _, _ = P.sb("pstage0", [128, NCH, PLE], F32)
    pstage1, _, _ = P.sb("pstage1", [128, NCH, PLE], F32)
    pstages = [pstage0, pstage1]
    pT, _, _ = P.sb("pT", [128, 2, TT], BF16)
    uf_tok, _, _ = P.sb("uf_tok", [128, NCH, 2, 4], F32)
    u_bf, _, _ = P.sb("u_bf", [128, NCH, 4], BF16)
    csb, _, _ = P.sb("csb", [128, NCH * 4], F32)
    kscale, _, _ = P.sb("kscale", [128, NCH * 4], F32)
    hist, _, _ = P.sb("hist", [128, NEC, 4], BF16)
    hist1, _, _ = P.sb("hist1", [128, NEC, 2], BF16)
    sm, _, _ = P.sb("sm", [128, 32], F32)
    bst, _, _ = P.sb("bst", [128, 12], F32)
    g4, _, _ = P.sb("g4", [4, 64], F32)
    rexp, _, _ = P.sb("rexp", [4, NCH, 4], F32)
    t2b, _, _ = P.sb("t2b", [128, TT], F32)
    print("SBUF used", P.sb_off - nc.sbuf_base, "of", P.sb_top - nc.sbuf_base)

    ps = [nc.alloc_psum_tensor(f"ps{i}", [128, 512], F32) for i in range(7)]
    psb = nc.alloc_psum_tensor("psb", [128, 1024], BF16)
    MAIN = [0, 1]
    UPD = [5, 6]
    MAIN_WIDE = [0, 1, 5, 6, 4]
    main_set = [MAIN]
    PS_S = 2
    PS_NS = [3, 4]
    PS_D = 2
    main_i = [0]
    upd_i = [0]

    def next_main():
        ms = main_set[0]
        i = ms[main_i[0] % len(ms)]
        main_i[0] += 1
        return ps[i], R("ps", i)

    def next_upd():
        i = UPD[upd_i[0] % len(UPD)]
        upd_i[0] += 1
        return ps[i], R("ps", i)

    def mm(out, lhsT, rhs, start, stop, reads, writes, inc):
        return P.op(PE, lambda e: e.matmul(out, lhsT=lhsT, rhs=rhs, start=start, stop=stop),
                    reads=reads, writes=writes, inc=inc)

    def tr(out, in_, ident, reads, writes, inc):
        return P.op(PE, lambda e: e.transpose(out=out, in_=in_, identity=ident),
                    reads=reads, writes=writes, inc=inc)

    def act(out, in_, func, reads, writes, bias=None, scale=None, E=None):
        kw = {}
        if bias is not None:
            kw["bias"] = bias
        if scale is not None:
            kw["scale"] = scale
        return P.op(ACT, lambda e: e.activation(out=out, in_=in_, func=func, **kw), reads=reads, writes=writes)

    def tt(E, out, in0, in1, op, reads, writes):
        return P.op(E, lambda e: e.tensor_tensor(out=out, in0=in0, in1=in1, op=op), reads=reads, writes=writes)

    def ts(E, out, in0, s1, op0, reads, writes, s2=None, op1=None):
        if op1 is None:
            return P.op(E, lambda e: e.tensor_scalar(out=out, in0=in0, scalar1=s1, scalar2=None, op0=op0),
                        reads=reads, writes=writes)
        return P.op(E, lambda e: e.tensor_scalar(out=out, in0=in0, scalar1=s1, scalar2=s2, op0=op0, op1=op1),
                    reads=reads, writes=writes)

    def stt(E, out, in0, scalar, in1, op0, op1, reads, writes):
        return P.op(E, lambda e: e.scalar_tensor_tensor(out=out, in0=in0, scalar=scalar, in1=in1, op0=op0, op1=op1),
                    reads=reads, writes=writes)

    def cp(E, out, in_, reads, writes):
        return P.op(E, lambda e: e.tensor_copy(out=out, in_=in_), reads=reads, writes=writes)

    def memset(E, ap, val, writes):
        return P.op(E, lambda e: e.memset(ap, val), writes=writes)

    r_const = R("const")
    P.dma(SP, par[:, :], par_d[:, :], s_const, writes=[r_const])
    s_bd = nc.alloc_semaphore("s_bd")
    s_wg = nc.alloc_semaphore("s_wg")
    r_bd = R("bd_w")
    r_wg = R("wg_w")
    P.dma(POOL, bd[:, :, :, :], bd_d.rearrange("p (m c n) -> p m c n", m=3, c=NEC), s_bd, writes=[r_bd])
    P.dma(POOL, wg[:, :, :], awin_d[:, 3 * E:3 * E + 8].rearrange("(c p) n -> p c n", p=128), s_wg, writes=[r_wg])

    r_id = R("ident")
    memset(POOL, ident_f[:, :], 0.0, [r_id])
    P.op(POOL, lambda e: e.affine_select(out=ident_f[:, :], in_=ident_f[:, :], pattern=[[-1, 128]],
                                         compare_op=ALU.not_equal, fill=1.0, base=0, channel_multiplier=1),
         reads=[r_id], writes=[r_id])
    cp(POOL, ident_b[:, :], ident_f[:, :], [r_id], [R("ident_b")])
    r_mask = R("mask")
    memset(POOL, onesD[:, :], 1.0, [R("onesD")])
    P.op(POOL, lambda e: e.affine_select(out=mask_b[:, :], in_=onesD[:, :], pattern=[[1, 128]],
                                         compare_op=ALU.is_ge, fill=0.0, base=0, channel_multiplier=-1),
         reads=[R("onesD")], writes=[r_mask])
    memset(POOL, ones_f[:, :], 1.0, [R("ones_f")])
    memset(POOL, onesD[:, :], 1.0 / D, [R("onesD")])
    memset(DVE, sm[:, 16:17], -0.5, [R("nhalf")])
    ts(DVE, par[:, O_GNH:O_GNH + 16], par[:, O_GN:O_GN + 16], 0.5, ALU.mult, [r_const], [R("gnh")])
    ts(DVE, par[0:4, O_BF:O_BF + 1], par[0:4, O_BF:O_BF + 1], -1.0, ALU.mult, [r_const], [R("negbf")])

    plan = []
    for h in range(H):
        plan.append(("a_xm", h, awin_d[:, h * DH:(h + 1) * DH], awin_b[:, h * DH:(h + 1) * DH], NDC, 512))
        plan.append(("a_o", h, awin_d[:, E + h * DH:E + (h + 1) * DH], awin_b[:, E + h * DH:E + (h + 1) * DH], NDC, 512))
        plan.append(("a_z", h, awin_d[:, 2 * E + h * DH:2 * E + (h + 1) * DH], awin_b[:, 2 * E + h * DH:2 * E + (h + 1) * DH], NDC, 512))

    def plan_tail(l, wout_b, wout_d):
        for dg in range(4):
            plan.append((f"out{l}", dg, wout_d[:, dg * 256:(dg + 1) * 256], wout_b[:, dg * 256:(dg + 1) * 256], NEC, 256))
        for hf in range(2):
            plan.append((f"pg{l}", hf, pleg_d[l * D:(l + 1) * D, hf * 512:(hf + 1) * 512], pleg_b[l * D:(l + 1) * D, hf * 512:(hf + 1) * 512], NDC, 512))
            plan.append((f"pw{l}", hf, plew_d[l * PLE:(l + 1) * PLE, hf * 512:(hf + 1) * 512], plew_b[l * PLE:(l + 1) * PLE, hf * 512:(hf + 1) * 512], 2, 512))

    plan_tail(0, awout_b, awout_d)
    for g in range(4):
        for nm, j in (("b_cg", 1), ("b_hx", 2), ("b_bg", 0), ("b_z", 3)):
            plan.append((nm, g, bwin_d[:, j * E + g * DH:j * E + (g + 1) * DH], bwin_b[:, j * E + g * DH:j * E + (g + 1) * DH], NDC, 512))
    plan_tail(1, bwout_b, bwout_d)
    nplan = len(plan)
    memset(POOL, CT[:, :, :, :], 0.0, [R("CT", h, k) for h in range(H) for k in range(4)])
    memset(POOL, CTb[:, :, :, :], 0.0, [R("CTb", h, k) for h in range(H) for k in range(4)])
    scr = []
    for (nm, idx, fsrc, bsrc, kc, ncol) in plan:
        rows = kc * 128
        step = rows if rows <= 1024 else rows // 2
        rl = []
        for j, r0 in enumerate(range(0, rows, step)):
            sem = nc.alloc_semaphore(f"s_prep_{nm}_{idx}_{j}")
            POOL.eng.dma_start(out=bsrc[r0:r0 + step, :], in_=fsrc[r0:r0 + step, :]).then_inc(sem, 16)
            P.dcnt[sem] = 16
            r = R("scr", nm, idx, j)
            r.w = (sem, 16)
            rl.append(r)
        scr.append(rl)
    total_uses = nplan * nt
    wstate = {"issued": 0, "next": 0, "rel": 0}

    def ring_issue():
        while wstate["issued"] < min(wstate["rel"] + NSLOT, total_uses):
            m = wstate["issued"]
            _, _, _, src, kc, ncol = plan[m % nplan]
            sl = m % NSLOT
            dst = ring[sl][:, 0:kc * ncol].rearrange("p (c n) -> p c n", c=kc)
            P.dma(SP, dst, src.rearrange("(c p) n -> p c n", p=128), s_ring[sl], reads=scr[m % nplan], writes=[R("ring", sl)])
            wstate["issued"] += 1

    def ring_get(name, idx):
        n = wstate["next"]
        key = plan[n % nplan]
        assert key[0] == name and key[1] == idx, (key[0], key[1], name, idx)
        ring_issue()
        assert wstate["issued"] > n, "ring deadlock"
        wstate["next"] = n + 1
        sl = n % NSLOT
        _, _, _, _, kc, ncol = key
        return ring[sl][:, 0:kc * ncol].rearrange("p (c n) -> p c n", c=kc), R("ring", sl)

    def ring_rel(k=1):
        wstate["rel"] += k
        assert wstate["rel"] <= wstate["next"]
        ring_issue()

    r_xres = P.Rs("xres", NDC)
    r_xT = P.Rs("xT", NDC)
    r_yT = P.Rs("yT", NEC)
    r_xin = R("x_in")
    r_ost = R("ostage")
    HEADBUFS = ["xmT", "xcT", "qT", "kT", "ktok", "vtok", "hnT", "sigo", "siluz"]

    def head_res():
        out = []
        for nm in HEADBUFS:
            out += P.Rs(nm, 4)
        out += [R("hn_tok"), R("PT"), R("t1"), R("t2", 0)]
        return out

    def l1_res():
        return P.Rs("uin", 4) + [R("cgs"), R("bgs"), R("szs"), R("bz")]

    def gate_res():
        return [R(n) for n in ("gi", "gL", "gNa", "gB", "gG", "gu", "gfl")]

    def ln_res2():
        return [R("meanb"), R("rstdb"), R("nbias"), R("sq", 0), R("sq", 1), R("gsb", 0), R("gsb", 1)]

    r_par = r_const
    evac_rr = [0]

    def evac_copy(out, in_, reads, writes, scale=None):
        evac_rr[0] += 1
        if scale is not None or evac_rr[0] % 2 == 0:
            return act(out, in_, AF.Copy, reads, writes, scale=scale)
        return cp(DVE, out, in_, reads, writes)

    def load_x(it):
        src = x_d[it * TT:(it + 1) * TT, :].rearrange("(c p) d -> p c d", p=128)
        P.transfer(r_yT, [r_xin])
        P.dma(SP, x_in[:, :, :], src, s_x, writes=[r_xin])

    def x_transposes():
        for dc in range(NDC):
            pt, rp = next_main()
            for c in range(NCH):
                tr(pt[:, c * 128:(c + 1) * 128], x_in[:, c, dc * 128:(dc + 1) * 128], ident_f[:, :],
                   [r_xin, r_id], [rp], inc=(c == NCH - 1))
            cp(DVE, xres[:, dc, :], pt[:, :], [rp], [r_xres[dc]])
            act(xT[:, dc, :], pt[:, :], AF.Copy, [rp], [r_xT[dc]])
        P.transfer([r_xin], r_yT)

    def issue_p(l, it):
        src = p_d[l, it * TT:(it + 1) * TT, :].rearrange("(c p) f -> p c f", p=128)
        P.dma(SP, pstages[l][:, :, :], src, s_p[l], writes=[R("pstage", l)])

    def load_p(l, it):
        pstage = pstages[l]
        for pc in range(2):
            pt, rp = next_main()
            for c in range(NCH):
                tr(pt[:, c * 128:(c + 1) * 128], pstage[:, c, pc * 128:(pc + 1) * 128], ident_f[:, :],
                   [R("pstage", l), r_id], [rp], inc=(c == NCH - 1))
            act(pT[:, pc, :], pt[:, :], AF.Copy, [rp], [R("pT", pc)])

    def reset_state(first=False):
        if not first:
            memset(POOL, CT[:, :, :, :], 0.0, [R("CT", h, k) for h in range(H) for k in range(4)])
            memset(POOL, CTb[:, :, :, :], 0.0, [R("CTb", h, k) for h in range(H) for k in range(4)])
        memset(DVE, nst[:, :, :], 0.0, [R("nst", h) for h in range(H)])
        memset(DVE, nbb[:, :, :], 0.0, [R("nbb", h) for h in range(H)])
        memset(DVE, hist[:, :, :], 0.0, P.Rs("hist", H))
        memset(DVE, hist1[:, :, :], 0.0, P.Rs("hist1", 4))
        memset(DVE, g4[:, 0:2], 0.0, [R("carry")])

    def gates_a():
        P.transfer(ln_res2(), gate_res())
        pi, rpi = next_main()
        pf, rpf = next_main()
        for dc in range(NDC):
            mm(pi[0:4, :], wg[:, dc, 0:4], xT[:, dc, :], dc == 0, dc == NDC - 1, [r_wg, r_xT[dc]], [rpi], inc=(dc == NDC - 1))
        for dc in range(NDC):
            mm(pf[0:4, :], wg[:, dc, 4:8], xT[:, dc, :], dc == 0, dc == NDC - 1, [r_wg, r_xT[dc]], [rpf], inc=(dc == NDC - 1))
        act(gi[:, :], pi[0:4, :], AF.Identity, [rpi, r_par], [R("gi")], bias=par[0:4, O_BI:O_BI + 1])
        act(gL[:, :], pf[0:4, :], AF.Exp, [rpf, R("negbf")], [R("gL")], bias=par[0:4, O_BF:O_BF + 1], scale=-1.0)
        act(gL[:, :], gL[:, :], AF.Ln, [R("gL")], [R("gL")], bias=1.0)
        rc = R("carry")
        P.op(DVE, lambda e: e.tensor_tensor_scan(out=gNa[:, :], data0=onesrow, data1=gL[:, :],
                                                 initial=g4[:, 0:1], op0=ALU.mult, op1=ALU.add),
             reads=[R("gL"), rc, R("ones_f")], writes=[R("gNa")])
        tt(DVE, gB[:, :], gi[:, :], gNa[:, :], ALU.add, [R("gi"), R("gNa")], [R("gB")])
        P.op(DVE, lambda e: e.tensor_tensor_scan(out=gG[:, :], data0=gB[:, :], data1=gB[:, :],
                                                 initial=g4[:, 1:2], op0=ALU.max, op1=ALU.max),
             reads=[R("gB"), rc], writes=[R("gG")])
        rg = R("g4s")
        cp(DVE, g4[:, 8:9], g4[:, 1:2], [rc], [rg])
        for c in range(1, NCH):
            cp(DVE, g4[:, 8 + c:9 + c], gG[:, c * 128 - 1:c * 128], [R("gG")], [rg])
        for c in range(NCH):
            cp(DVE, g4[:, 12 + c:13 + c], gG[:, c * 128 + 127:c * 128 + 128], [R("gG")], [rg])
        ts(DVE, g4[:, 16:20], g4[:, 8:12], -1.0, ALU.mult, [rg], [rg])
        tt(DVE, g4[:, 24:28], g4[:, 8:12], g4[:, 12:16], ALU.subtract, [rg], [rg])
        act(g4[:, 20:24], g4[:, 24:28], AF.Exp, [rg], [rg])
        for c in range(NCH):
            sl = slice(c * 128, (c + 1) * 128)
            act(gu[:, sl], gB[:, sl], AF.Exp, [R("gB"), rg], [R("gu")], bias=g4[:, 16 + c:17 + c])
            act(gfl[:, sl], gNa[:, sl], AF.Exp, [R("gNa"), rg], [R("gfl")], bias=g4[:, 16 + c:17 + c])
        cp(DVE, g4[:, 0:1], gNa[:, TT - 1:TT], [R("gNa")], [rc])
        cp(DVE, g4[:, 1:2], gG[:, TT - 1:TT], [R("gG")], [rc])
    def gates_b():
        rg = R("g4s")
        pt, rp = next_main()
        for c in range(NCH):
            sl = slice(c * 128, (c + 1) * 128)
            tr(pt[:, c * 8:c * 8 + 4], gu[:, sl], ident_f[0:4, 0:4], [R("gu"), r_id], [rp], inc=False)
            tr(pt[:, c * 8 + 4:c * 8 + 8], gfl[:, sl], ident_f[0:4, 0:4], [R("gfl"), r_id], [rp], inc=(c == NCH - 1))
        cp(DVE, uf_tok[:, :, :, :], pt[:, 0:NCH * 8].rearrange("p (c t h) -> p c t h", c=NCH, t=2), [rp], [R("uf_tok")])
        cp(DVE, u_bf[:, :, :], uf_tok[:, :, 0, :], [R("uf_tok")], [R("u_bf")])
        for c in range(NCH):
            ts(DVE, rexp[:, c, :], ident_f[0:4, 0:4], g4[:, 20 + c:21 + c], ALU.mult, [rg, r_id], [R("rexp")])
        pt2, rp2 = next_main()
        mm(pt2[:, 0:NCH * 4], ones_f[0:4, :], rexp[:, :, :].rearrange("p c h -> p (c h)"), True, True,
           [R("ones_f"), R("rexp")], [rp2], inc=True)
        cp(DVE, csb[:, :], pt2[:, 0:NCH * 4], [rp2], [R("csb")])
        ts(DVE, kscale[:, :], csb[:, :], KSC, ALU.mult, [R("csb")], [R("kscale")])
        P.dump("gu", gu[:, :], [4, TT], rd=[R("gu")])
        P.dump("gfl", gfl[:, :], [4, TT], rd=[R("gfl")])
        P.dump("csb", csb[:, :], [128, 16], rd=[R("csb")])
        P.dump("uf_tok", uf_tok[:, :, :, :], [128, NCH, 2, 4], rd=[R("uf_tok")])

    onesrow = None
    cur = {"it": 0}

    def proj_group(w, rw, dst_fn):
        for ec in range(4):
            pt, rp = next_main()
            for dc in range(NDC):
                mm(pt[:, :], w[:, dc, ec * 128:(ec + 1) * 128], xT[:, dc, :], dc == 0, dc == NDC - 1,
                   [rw, r_xT[dc]], [rp], inc=(dc == NDC - 1))
            dst_fn(ec, pt, rp)

    def head_A(h):
        w, rw = ring_get("a_xm", h)
        rxm = P.Rs("xmT", 4)
        rxc = P.Rs("xcT", 4)
        for ec in range(4):
            cp(DVE, xmT[:, ec, 0:3], hist[:, 4 * h + ec, 0:3], [R("hist", h)], [rxm[ec]])

        def ev_xm(ec, pt, rp):
            act(xmT[:, ec, 3:3 + TT], pt[:, :], AF.Copy, [rp], [rxm[ec]])
        proj_group(w, rw, ev_xm)
        ring_rel()
        for ec in range(4):
            cp(DVE, hist[:, 4 * h + ec, 0:3], xmT[:, ec, TT:TT + 3], [rxm[ec]], [R("hist", h)])

    def build_diag(h):
        rdg = R("diag")
        for ec in range(4):
            for k in range(4):
                c0 = O_CWA + (4 * h + ec) * 4 + k
                ts(DVE, diag[:, k, ec, :], ident_b[:, :], par[:, c0:c0 + 1], ALU.mult, [R("ident_b"), r_par], [rdg])

    def head_A1b(h):
        rxm = P.Rs("xmT", 4)
        rxc = P.Rs("xcT", 4)
        rdg = R("diag")
        for ec in range(4):
            pt, rp = next_main()
            for k in range(4):
                mm(pt[:, :], diag[:, k, ec, :], xmT[:, ec, k:k + TT], k == 0, k == 3, [rdg, rxm[ec]], [rp], inc=(k == 3))
            c0 = O_CBA + 4 * h + ec
            act(xcT[:, ec, :], pt[:, :], AF.Silu, [rp, r_par], [rxc[ec]], bias=par[:, c0:c0 + 1])
        for ec in range(4):
            pt, rp = next_main()
            mm(pt[:, :], bd[:, 0, 4 * h + ec, :], xcT[:, ec, :], True, True, [r_bd, rxc[ec]], [rp], inc=True)
            evac_copy(qT[:, ec, :], pt[:, :], [rp], [R("qT", ec)])
            pt, rp = next_main()
            mm(pt[:, :], bd[:, 1, 4 * h + ec, :], xcT[:, ec, :], True, True, [r_bd, rxc[ec]], [rp], inc=True)
            act(kT[:, ec, :], pt[:, :], AF.Copy, [rp], [R("kT", ec)], scale=KSC)
    def head_A2(h):
        rxm = P.Rs("xmT", 4)
        rxc = P.Rs("xcT", 4)
        for c in range(NCH):
            pt, rp = next_main()
            for ec in range(4):
                mm(pt[:, ec * 128:(ec + 1) * 128], xcT[:, ec, c * 128:(c + 1) * 128], bd[:, 1, 4 * h + ec, :], True, True,
                   [r_bd, rxc[ec]], [rp], inc=(ec == 3))
            act(ktok[:, c, :], pt[:, :], AF.Copy, [rp, R("kscale")], [R("ktok", c)], scale=kscale[:, c * 4 + h:c * 4 + h + 1])
            pt, rp = next_main()
            for ec in range(4):
                mm(pt[:, ec * 128:(ec + 1) * 128], xmT[:, ec, 3 + c * 128:3 + (c + 1) * 128], bd[:, 2, 4 * h + ec, :], True, True,
                   [r_bd, rxm[ec]], [rp], inc=(ec == 3))
            act(vtok[:, c, :], pt[:, :], AF.Copy, [rp, R("uf_tok")], [R("vtok", c)], scale=uf_tok[:, c, 0, h:h + 1])
        if h == 0:
            P.dump("xmT", xmT[:, :, :], [128, 4, TT + 4], BF16, rd=rxm)
            P.dump("xcT", xcT[:, :, :], [128, 4, TT], BF16, rd=rxc)
            P.dump("qT", qT[:, :, :], [128, 4, TT], BF16, rd=P.Rs("qT", 4))
            P.dump("kT", kT[:, :, :], [128, 4, TT], BF16, rd=P.Rs("kT", 4))
            P.dump("ktok", ktok[:, :, :], [128, NCH, DH], BF16, rd=P.Rs("ktok", 4))
            P.dump("vtok", vtok[:, :, :], [128, NCH, DH], BF16, rd=P.Rs("vtok", 4))

    def mlstm_part1(h, c):
        sl = slice(c * 128, (c + 1) * 128)
        par_ = c % 2
        rq = P.Rs("qT", 4)
        rk = P.Rs("kT", 4)
        pS = ps[PS_S]
        pD = ps[PS_D]
        rS, rD = R("ps", PS_S), R("ps", PS_D)
        rUn = rD
        pN, rN = ps[PS_NS[par_]], R("ps", PS_NS[par_])
        rs = R("sm", par_)
        o = 8 * par_
        for dk in range(4):
            mm(pS[:, 0:128], kT[:, dk, sl], qT[:, dk, sl], dk == 0, dk == 3, [rk[dk], rq[dk]], [rS], inc=(dk == 3))
        tt(DVE, PT[:, :], pS[:, 0:128], mask_b[:, :], ALU.mult, [rS, r_mask], [R("PT")])
        mm(pN[:, :], PT[:, :], vtok[:, c, :], True, False, [R("PT"), R("vtok", c)], [rN], inc=False)
        for dk in range(4):
            mm(pN[:, :], qT[:, dk, sl], CTb[:, h, dk, :], False, dk == 3, [rq[dk], R("CTb", h, dk)], [rN], inc=(dk == 3))
        mm(pD[:, 256:257], PT[:, :], u_bf[:, c, h:h + 1], True, False, [R("PT"), R("u_bf")], [rD], inc=False)
        for dk in range(4):
            mm(pD[:, 256:257], qT[:, dk, sl], nbb[:, h, dk:dk + 1], False, dk == 3, [rq[dk], R("nbb", h)], [rD], inc=(dk == 3))
        cp(DVE, sm[:, o + 7:o + 8], pD[:, 256:257], [rD], [rs])
        cs_col = csb[:, c * 4 + h:c * 4 + h + 1]
        for dk in range(4):
            pu, ru = next_upd()
            mm(pu[:, :], ktok[:, c, dk * 128:(dk + 1) * 128], vtok[:, c, :], True, True, [R("ktok", c), R("vtok", c)], [ru], inc=True)
            stt(DVE, CT[:, h, dk, :], CT[:, h, dk, :], cs_col, pu[:, :], ALU.mult, ALU.add, [R("CT", h, dk), R("csb"), ru], [R("CT", h, dk)])
            act(CTb[:, h, dk, :], CT[:, h, dk, :], AF.Copy, [R("CT", h, dk)], [R("CTb", h, dk)])
        for dk in range(4):
            mm(pD[:, 260 + dk:261 + dk], ktok[:, c, dk * 128:(dk + 1) * 128], u_bf[:, c, h:h + 1], True, True,
               [R("ktok", c), R("u_bf")], [rUn], inc=(dk == 3))
        stt(DVE, nst[:, h, :], nst[:, h, :], cs_col, pD[:, 260:264], ALU.mult, ALU.add, [R("nst", h), R("csb"), rUn], [R("nst", h)])
        cp(DVE, nbb[:, h, :], nst[:, h, :], [R("nst", h)], [R("nbb", h)])

    def mlstm_part2(h, c):
        sl = slice(c * 128, (c + 1) * 128)
        par_ = c % 2
        pN, rN = ps[PS_NS[par_]], R("ps", PS_NS[par_])
        rs = R("sm", par_)
        o = 8 * par_
        rbst = R("bst", par_)
        bs = bst[:, 6 * par_:6 * par_ + 6]
        stt(DVE, sm[:, o:o + 1], sm[:, o + 7:o + 8], -1.0, sm[:, o + 7:o + 8], ALU.mult, ALU.max, [rs], [rs])
        tt(DVE, sm[:, o:o + 1], sm[:, o:o + 1], uf_tok[:, c, 1, h:h + 1], ALU.max, [rs, R("uf_tok")], [rs])
        P.op(DVE, lambda e: e.bn_stats(out=bs, in_=pN[:, :]), reads=[rN], writes=[rbst])
        P.op(DVE, lambda e: e.bn_aggr(out=sm[:, o + 2:o + 4], in_=bs), reads=[rbst], writes=[rs])
        tt(DVE, sm[:, o + 1:o + 2], sm[:, o:o + 1], sm[:, o:o + 1], ALU.mult, [rs], [rs])
        stt(DVE, sm[:, o + 4:o + 5], sm[:, o + 1:o + 2], GN_EPS, sm[:, o + 3:o + 4], ALU.mult, ALU.add, [rs], [rs])
        if cur["it"] == 0:
            act(sm[:, o + 5:o + 6], sm[:, o + 4:o + 5], AF.Sqrt, [rs], [rs])
            P.op(DVE, lambda e: e.reciprocal(out=sm[:, o + 6:o + 7], in_=sm[:, o + 5:o + 6]), reads=[rs], writes=[rs])
        else:
            tt(POOL, sm[:, o + 6:o + 7], sm[:, o + 4:o + 5], sm[:, 16:17], ALU.pow, [rs, R("nhalf")], [rs])
        ts(DVE, hn_tok[:, :], pN[:, :], sm[:, o + 2:o + 3], ALU.subtract, [rN, rs], [R("hn_tok")], s2=sm[:, o + 6:o + 7], op1=ALU.mult)

    def mlstm_part2b(h, c):
        sl = slice(c * 128, (c + 1) * 128)
        rb = R("psb", 0)
        for ec in range(4):
            tr(psb[:, ec * 128:(ec + 1) * 128], hn_tok[:, ec * 128:(ec + 1) * 128], ident_b[:, :], [R("hn_tok"), R("ident_b")], [rb], inc=(ec == 3))
        for ec in range(4):
            e = 4 * h + ec
            act(hnT[:, ec, sl], psb[:, ec * 128:(ec + 1) * 128], AF.Copy, [rb, R("gnh")], [R("hnT", ec)], scale=par[:, O_GNH + e:O_GNH + e + 1])

    def head_C_proj(h, which, ecs):
        w, rw = which
        for ec in ecs:
            pt, rp = next_main()
            for dc in range(NDC):
                mm(pt[:, :], w[:, dc, ec * 128:(ec + 1) * 128], xT[:, dc, :], dc == 0, dc == NDC - 1, [rw, r_xT[dc]], [rp], inc=(dc == NDC - 1))
            yield ec, pt, rp

    def head_BC(h):
        wo = ring_get("a_o", h)
        wz = ring_get("a_z", h)
        pieces = [(wo, [0, 1], True), (wo, [2, 3], True), (wz, [0, 1], False), (wz, [2, 3], False)]
        def piece(c):
            wv, ecs, is_o = pieces[c]
            for ec, pt, rp in head_C_proj(h, wv, ecs):
                if is_o:
                    act(sigo[:, ec, :], pt[:, :], AF.Tanh, [rp], [R("sigo", ec)], scale=0.5)
                else:
                    act(siluz[:, ec, :], pt[:, :], AF.Silu, [rp], [R("siluz", ec)])
            if c % 2 == 1:
                ring_rel()
        mlstm_part1(h, 0)
        piece(0)
        mlstm_part1(h, 1)
        mlstm_part2(h, 0)
        piece(1)
        mlstm_part2b(h, 0)
        mlstm_part1(h, 2)
        mlstm_part2(h, 1)
        piece(2)
        mlstm_part2b(h, 1)
        mlstm_part1(h, 3)
        mlstm_part2(h, 2)
        piece(3)
        mlstm_part2b(h, 2)
        mlstm_part2(h, 3)

    def head_BC_fin(h):
        mlstm_part2b(h, 3)
        for ec in range(4):
            e = 4 * h + ec
            t2x = t2 if ec % 2 == 0 else t2b
            rt2 = R("t2", ec % 2)
            stt(DVE, t1[:, :], sigo[:, ec, :], 1.0, hnT[:, ec, :], ALU.add, ALU.mult, [R("hnT", ec), R("sigo", ec)], [R("t1")])
            stt(DVE, t2x[:, :], xcT[:, ec, :], par[:, O_SKIP + e:O_SKIP + e + 1], t1[:, :], ALU.mult, ALU.add,
                [R("xcT", ec), R("t1"), r_par], [rt2])
            tt(DVE if cur["it"] == 0 else POOL, yT[:, e, :], t2x[:, :], siluz[:, ec, :], ALU.mult, [rt2, R("siluz", ec)], [r_yT[e]])
        if h == 0:
            P.dump("hnT", hnT[:, :, :], [128, 4, TT], BF16, rd=P.Rs("hnT", 4))
            P.dump("sigo", sigo[:, :, :], [128, 4, TT], BF16, rd=P.Rs("sigo", 4))
            P.dump("siluz", siluz[:, :, :], [128, 4, TT], BF16, rd=P.Rs("siluz", 4))
            P.dump("t1", t1[:, :], [128, TT], F32, rd=[R("t1")])
            P.dump("yT0", yT[:, 0:4, :], [128, 4, TT], BF16, rd=r_yT[0:4])

    def tail(l, it):
        load_p(l, it)
        if l == 0:
            P.transfer(gate_res(), ln_res2())
        pm, rpm = ps[PS_S], R("ps", PS_S)
        pq, rpq = ps[PS_NS[0]], R("ps", PS_NS[0])

        def ln_stats(dc):
            b = dc % 2
            act(sq[b][:, :], xres[:, dc, :], AF.Square, [r_xres[dc]], [R("sq", b)])
            mm(pm[:, :], onesD[:, :], xres[:, dc, :], dc == 0, dc == NDC - 1, [R("onesD"), r_xres[dc]], [rpm], inc=(dc == NDC - 1))
            mm(pq[:, :], onesD[:, :], sq[b][:, :], dc == 0, dc == NDC - 1, [R("onesD"), R("sq", b)], [rpq], inc=True)

        pending = None
        for dg in range(4):
            w, rw = ring_get(f"out{l}", dg)
            for j in range(2):
                dc = 2 * dg + j
                pt, rp = next_main()
                for ec in range(NEC):
                    mm(pt[:, :], w[:, ec, j * 128:(j + 1) * 128], yT[:, ec, :], ec == 0, ec == NEC - 1, [rw, r_yT[ec]], [rp], inc=(ec == NEC - 1))
                stt(DVE, xres[:, dc, :], xres[:, dc, :], ALPHA, pt[:, :], ALU.mult, ALU.add, [r_xres[dc], rp], [r_xres[dc]])
                if pending is not None:
                    ln_stats(pending)
                pending = dc
            ring_rel()
        if l == 1 and it + 1 < nt:
            load_x(it + 1)
        ln_stats(pending)
        act(meanb[:, :], pm[:, :], AF.Copy, [rpm], [R("meanb")])
        tt(DVE, nbias[:, :], meanb[:, :], meanb[:, :], ALU.mult, [R("meanb")], [R("nbias")])
        stt(DVE, rstdb[:, :], pq[:, :], LN_EPS, nbias[:, :], ALU.add, ALU.subtract, [rpq, R("nbias")], [R("rstdb")])
        act(rstdb[:, :], rstdb[:, :], AF.Sqrt, [R("rstdb")], [R("rstdb")])
        P.op(DVE, lambda e: e.reciprocal(out=rstdb[:, :], in_=rstdb[:, :]), reads=[R("rstdb")], writes=[R("rstdb")])
        tt(DVE, nbias[:, :], meanb[:, :], rstdb[:, :], ALU.mult, [R("meanb"), R("rstdb")], [R("nbias")])
        for dc in range(NDC):
            tt(DVE, xres[:, dc, :], xres[:, dc, :], rstdb[:, :], ALU.mult, [r_xres[dc], R("rstdb")], [r_xres[dc]])
            tt(DVE, xres[:, dc, :], xres[:, dc, :], nbias[:, :], ALU.subtract, [r_xres[dc], R("nbias")], [r_xres[dc]])
            cg_ = O_LNG + l * 8 + dc
            cb_ = O_LNB + l * 8 + dc
            act(xT[:, dc, :], xres[:, dc, :], AF.Identity, [r_xres[dc], r_par], [r_xT[dc]], bias=par[:, cb_:cb_ + 1], scale=par[:, cg_:cg_ + 1])
            ts(POOL, xres[:, dc, :], xres[:, dc, :], par[:, cg_:cg_ + 1], ALU.mult, [r_xres[dc], r_par], [r_xres[dc]], s2=par[:, cb_:cb_ + 1], op1=ALU.add)
        if l == 0 and it == 0:
            P.dump("xln0", xres[:, :, :], [128, NDC, TT], rd=r_xres)
        for hf in range(2):
            wG, rwG = ring_get(f"pg{l}", hf)
            wP, rwP = ring_get(f"pw{l}", hf)
            for j in range(4):
                dc = 4 * hf + j
                b = j % 2
                pt, rp = next_main()
                for di in range(NDC):
                    mm(pt[:, :], wG[:, di, j * 128:(j + 1) * 128], xT[:, di, :], di == 0, di == NDC - 1, [rwG, r_xT[di]], [rp], inc=(di == NDC - 1))
                act(gsb[b][:, :], pt[:, :], AF.Tanh, [rp], [R("gsb", b)], scale=0.5)
                pt2, rp2 = next_main()
                for pc in range(2):
                    mm(pt2[:, :], wP[:, pc, j * 128:(j + 1) * 128], pT[:, pc, :], pc == 0, pc == 1, [rwP, R("pT", pc)], [rp2], inc=(pc == 1))
                stt(DVE, gsb[b][:, :], gsb[b][:, :], 1.0, pt2[:, :], ALU.add, ALU.mult, [R("gsb", b), rp2], [R("gsb", b)])
                stt(DVE, xres[:, dc, :], gsb[b][:, :], 0.5, xres[:, dc, :], ALU.mult, ALU.add, [r_xres[dc], R("gsb", b)], [r_xres[dc]])
            ring_rel(2)
        if l == 0:
            for dc in range(NDC):
                if dc % 2:
                    cp(DVE, xT[:, dc, :], xres[:, dc, :], [r_xres[dc]], [r_xT[dc]])
                else:
                    act(xT[:, dc, :], xres[:, dc, :], AF.Copy, [r_xres[dc]], [r_xT[dc]])
        if l == 0 and it == 0:
            P.dump("x1", xres[:, :, :], [128, NDC, TT], rd=r_xres)

    def layer1(it):
        P.transfer(head_res(), l1_res())
        ruin = P.Rs("uin", 4)
        for g in range(4):
            wcg = ring_get("b_cg", g)
            whx = ring_get("b_hx", g)
            for ec in range(4):
                cp(DVE, uin[:, ec, 0:2], hist1[:, 4 * g + ec, 0:2], [R("hist1", g)], [ruin[ec]])
            rdg = R("diag")
            for ec in range(4):
                for k in range(3):
                    c0 = O_CWB + (4 * g + ec) * 3 + k
                    ts(DVE, diag[:, k, ec, :], ident_b[:, :], par[:, c0:c0 + 1], ALU.mult, [R("ident_b"), r_par], [rdg])

            def mmproj(wv, ec):
                w, rw = wv
                pt, rp = next_main()
                for dc in range(NDC):
                    mm(pt[:, :], w[:, dc, ec * 128:(ec + 1) * 128], xT[:, dc, :], dc == 0, dc == NDC - 1, [rw, r_xT[dc]], [rp], inc=(dc == NDC - 1))
                return pt, rp

            for ec in range(4):
                pt, rp = mmproj(wcg, ec)
                act(cgs[:, :], pt[:, :], AF.Copy, [rp], [R("cgs")])
                pt, rp = mmproj(whx, ec)
                tt(DVE, uin[:, ec, 2:2 + TT], pt[:, :], cgs[:, :], ALU.mult, [rp, R("cgs")], [ruin[ec]])
            ring_rel(2)
            for ec in range(4):
                cp(DVE, hist1[:, 4 * g + ec, 0:2], uin[:, ec, TT:TT + 2], [ruin[ec]], [R("hist1", g)])
            wbg = ring_get("b_bg", g)
            wz = ring_get("b_z", g)
            for ec in range(4):
                e = 4 * g + ec
                pt, rp = mmproj(wbg, ec)
                act(bgs[:, :], pt[:, :], AF.Copy, [rp], [R("bgs")])
                pt, rp = mmproj(wz, ec)
                act(szs[:, :], pt[:, :], AF.Silu, [rp], [R("szs")])
                tt(POOL, bz[:, :], bgs[:, :], szs[:, :], ALU.mult, [R("bgs"), R("szs")], [R("bz")])
                pu, rpu = next_main()
                for k in range(3):
                    mm(pu[:, :], diag[:, k, ec, :], uin[:, ec, k:k + TT], k == 0, k == 2, [rdg, ruin[ec]], [rpu], inc=(k == 2))
                tt(DVE, yT[:, e, :], pu[:, :], bz[:, :], ALU.mult, [rpu, R("bz")], [r_yT[e]])
            ring_rel(2)
        if it == 0:
            P.dump("yT1", yT[:, 0:4, :], [128, 4, TT], BF16, rd=r_yT[0:4])

    def store_out(it):
        P.transfer(head_res() + l1_res(), [r_ost])
        for c in range(NCH):
            for hb in range(2):
                pt, rp = next_main()
                for j in range(4):
                    dc = hb * 4 + j
                    tr(pt[:, j * 128:(j + 1) * 128], xres[:, dc, c * 128:(c + 1) * 128], ident_f[:, :], [r_xres[dc], r_id], [rp], inc=(j == 3))
                evac_copy(ostage[:, c, hb * 512:(hb + 1) * 512], pt[:, :], [rp], [r_ost])
        dst = out_d[it * TT:(it + 1) * TT, :].rearrange("(c p) d -> p c d", p=128)
        P.dma(SP, dst, ostage[:, :, :], s_o, reads=[r_ost])
        P.transfer([r_ost], head_res())

    onesrow = ones_f[0:4, 0:1].to_broadcast([4, TT])
    load_x(0)
    for it in range(nt):
        cur["it"] = it
        if it % TPS == 0:
            reset_state(first=(it == 0))
        x_transposes()
        if it == 0:
            P.dump("xT", xT[:, :, :], [128, NDC, TT], BF16, rd=r_xT)
        issue_p(0, it)
        issue_p(1, it)
        gates_a()
        build_diag(0)
        head_A(0)
        for h in range(H):
            head_A1b(h)
            if h == 0:
                gates_b()
            head_A2(h)
            if h + 1 < H:
                build_diag(h + 1)
            head_BC(h)
            if h + 1 < H:
                head_A(h + 1)
            head_BC_fin(h)
        main_set[0] = MAIN_WIDE
        tail(0, it)
        layer1(it)
        tail(1, it)
        store_out(it)
        main_set[0] = MAIN
    SP.eng.wait_ge(s_o, P.dcnt[s_o])
    if P.dcnt.get(P.s_dbg, 0):
        SP.eng.wait_ge(P.s_dbg, P.dcnt[P.s_dbg])
    print("instructions emitted:", P.n_inst, "PE", PE.cnt, "ACT", ACT.cnt, "DVE", DVE.cnt, "POOL", POOL.cnt)
    return P


def _host_layout(inputs):
    f = np.float32
    a_conv_w = np.asarray(inputs["a_conv_w"], f)[0]
    b_conv_w = np.asarray(inputs["b_conv_w"], f)[0]
    par = np.zeros((128, NPAR), f)
    par[:, O_CWA:O_CWA + 64] = a_conv_w.reshape(4, NEC, 128).transpose(2, 1, 0).reshape(128, 64)
    par[:, O_CBA:O_CBA + 16] = np.asarray(inputs["a_conv_b"], f)[0].reshape(NEC, 128).T
    par[:, O_GN:O_GN + 16] = np.asarray(inputs["a_gn_w"], f)[0].reshape(NEC, 128).T
    par[:, O_SKIP:O_SKIP + 16] = np.asarray(inputs["a_skip"], f)[0].reshape(NEC, 128).T
    par[:, O_CWB:O_CWB + 48] = b_conv_w.reshape(3, NEC, 128).transpose(2, 1, 0).reshape(128, 48)
    par[:, O_LNG:O_LNG + 16] = np.asarray(inputs["ln_g"], f).reshape(2, NDC, 128).transpose(2, 0, 1).reshape(128, 16)
    par[:, O_LNB:O_LNB + 16] = np.asarray(inputs["ln_b"], f).reshape(2, NDC, 128).transpose(2, 0, 1).reshape(128, 16)
    par[0:4, O_BI] = np.asarray(inputs["a_b_i"], f)[0]
    par[0:4, O_BF] = np.asarray(inputs["a_b_f"], f)[0]
    bd = np.zeros((128, 3, NEC, 128), f)
    pidx = np.arange(128)
    for m, nm in enumerate(("a_w_q", "a_w_k", "a_w_v")):
        w = np.asarray(inputs[nm], f)[0].reshape(NEC, 32, 4, 4)
        for o in range(4):
            bd[pidx, m, :, (pidx // 4) * 4 + o] = w[:, pidx // 4, pidx % 4, o].T
    return par, np.ascontiguousarray(bd.reshape(128, 3 * NEC * 128))


_CACHE = {}


def kernel(**inputs):
    nt = int(os.environ.get("MK_NT", NT_FULL))
    dbg = tuple(x for x in os.environ.get("MK_DEBUG", "").split(",") if x)
    key = (nt, dbg)
    if key not in _CACHE:
        _CACHE[key] = build(nt, dbg)
    P = _CACHE[key]
    f = np.float32
    x = np.asarray(inputs["x"], f)
    p = np.asarray(inputs["p"], f)
    par, bd = _host_layout(inputs)
    shared = {
        "a_w_in": np.ascontiguousarray(np.asarray(inputs["a_w_in"], f)[0]),
        "a_w_out": np.ascontiguousarray(np.asarray(inputs["a_w_out"], f)[0]),
        "b_w_in": np.ascontiguousarray(np.asarray(inputs["b_w_in"], f)[0]),
        "b_w_out": np.ascontiguousarray(np.asarray(inputs["b_w_out"], f)[0]),
        "ple_w": np.ascontiguousarray(np.asarray(inputs["ple_w"], f).reshape(2 * PLE, D)),
        "ple_gate_w": np.ascontiguousarray(np.asarray(inputs["ple_gate_w"], f).reshape(2 * D, D)),
        "bd": bd,
        "params": par,
    }
    in_maps = []
    for c in range(NCORES):
        m = dict(shared)
        m["x"] = np.ascontiguousarray(x[c * BPC:(c + 1) * BPC].reshape(TOK, D))
        m["p"] = np.ascontiguousarray(p[:, c * BPC:(c + 1) * BPC].reshape(2, TOK, PLE))
        in_maps.append(m)
    res = run_bass_kernel_spmd(P.nc, in_maps, core_ids=list(range(NCORES)))
    kernel.last_results = res
    out = np.stack([np.asarray(r["out"], f).reshape(BPC, SEQ, D) for r in res.results], 0)
    return out.reshape(NCORES * BPC, SEQ, D)
```

```python
import os
import numpy as np
import concourse.bass as bass
import concourse.mybir as mybir
from concourse.bass_utils import run_bass_kernel_spmd

F32 = mybir.dt.float32
BF16 = mybir.dt.bfloat16
AF = mybir.ActivationFunctionType
ALU = mybir.AluOpType

NCORES = 8
D = 1024
E = 2048
H = 4
DH = 512
PLE = 256
SEQ = 2048
BPC = 2
TOK = BPC * SEQ
TT = 512
NCH = TT // 128
TPS = SEQ // TT
NT_FULL = TOK // TT
NDC = D // 128
NEC = E // 128
NA_IN = 3 * E + 2 * H
NB_IN = 4 * E
ALPHA = float((2.0 * 2) ** 0.25)
LN_EPS = 1e-5
GN_EPS = 1e-6
KSC = float(DH ** -0.5)

O_CWA, O_CBA, O_GN, O_SKIP, O_CWB, O_LNG, O_LNB, O_BI, O_BF = 0, 64, 80, 96, 112, 160, 176, 192, 193
O_GNH = 194
NPAR = 210
NSLOT = 4
NOSELF = False
SLOT_BYTES = 8192


class Res:
    __slots__ = ("name", "w", "r", "excl")

    def __init__(self, name):
        self.name = name
        self.w = None
        self.r = {}
        self.excl = isinstance(name, tuple) and name[0] in ("ps", "psb")


class Eng:
    def __init__(self, name, eng, sem, is_pe=False):
        self.name = name
        self.eng = eng
        self.sem = sem
        self.is_pe = is_pe
        self.cnt = 0
        self.seen = {}


class Prog:
    def __init__(self, nt=NT_FULL, debug=()):
        self.nt = nt
        self.debug = set(debug)
        self.dbg_outs = {}
        nc = self.nc = bass.Bass("TRN2", target_bir_lowering=False)
        self.PE = Eng("pe", nc.tensor, nc.alloc_semaphore("s_pe"), True)
        self.ACT = Eng("act", nc.scalar, nc.alloc_semaphore("s_act"))
        self.DVE = Eng("dve", nc.vector, nc.alloc_semaphore("s_dve"))
        self.POOL = Eng("pool", nc.gpsimd, nc.alloc_semaphore("s_pool"))
        self.SP = Eng("sp", nc.sync, nc.alloc_semaphore("s_sp"))
        self.res = {}
        self.dcnt = {}
        self.sb_off = (nc.sbuf_base + 63) // 64 * 64
        self.sb_top = nc.sbuf_top
        self.n_inst = 0

    def R(self, *k):
        r = self.res.get(k)
        if r is None:
            r = self.res[k] = Res(k)
        return r

    def Rs(self, name, n):
        return [self.R(name, i) for i in range(n)]

    def _wait(self, E, reads, writes):
        need = {}

        def upd(tok):
            s, v = tok
            if need.get(s, 0) < v:
                need[s] = v

        for r in reads:
            if r.w is not None:
                upd(r.w)
        for w in writes:
            if w.w is not None:
                upd(w.w)
            for s, v in w.r.items():
                upd((s, v))
        for s, v in need.items():
            if s == E.sem and (E.is_pe or NOSELF):
                continue
            if E.seen.get(s, 0) >= v:
                continue
            E.eng.wait_ge(s, v)
            E.seen[s] = v

    def op(self, E, fn, reads=(), writes=(), inc=True):
        writes = list(writes) + [r for r in reads if r.excl]
        reads = [r for r in reads if not r.excl]
        self._wait(E, reads, writes)
        ins = fn(E.eng)
        self.n_inst += 1
        if inc:
            E.cnt += 1
            ins.then_inc(E.sem, 1)
            v = E.cnt
        else:
            assert E.is_pe
            v = E.cnt + 1
        for r in reads:
            if r.r.get(E.sem, 0) < v:
                r.r[E.sem] = v
        for w in writes:
            w.w = (E.sem, v)
            w.r = {}
        return ins

    def dma(self, Q, out, in_, sem, reads=(), writes=()):
        writes = list(writes) + [r for r in reads if r.excl]
        reads = [r for r in reads if not r.excl]
        self._wait(Q, reads, writes)
        Q.eng.dma_start(out=out, in_=in_).then_inc(sem, 16)
        self.n_inst += 1
        v = self.dcnt[sem] = self.dcnt.get(sem, 0) + 16
        for r in reads:
            r.r[sem] = v
        for w in writes:
            w.w = (sem, v)
            w.r = {}

    def transfer(self, olds, news):
        m = {}
        for o in olds:
            if o.w is not None:
                s, v = o.w
                if m.get(s, 0) < v:
                    m[s] = v
            for s, v in o.r.items():
                if m.get(s, 0) < v:
                    m[s] = v
        for n in news:
            for s, v in m.items():
                if n.r.get(s, 0) < v:
                    n.r[s] = v

    def sb(self, name, shape, dt, at=None):
        nb = int(np.prod(shape[1:])) * (4 if dt == F32 else 2)
        nb = (nb + 63) // 64 * 64
        if at is None:
            off = self.sb_off
            self.sb_off += nb
            assert self.sb_off <= self.sb_top, (name, self.sb_off, self.sb_top)
        else:
            off = at
        t = self.nc.alloc_sbuf_tensor_at(name, list(shape), dt, offset=off)
        return t, off, nb

    def dump(self, name, ap, shape, dt=F32, rd=()):
        if name not in self.debug or name in self.dbg_outs:
            return
        nc = self.nc
        o = nc.dram_tensor("dbg_" + name, list(shape), dt, kind="ExternalOutput").ap()
        self.dbg_outs[name] = (list(shape), dt)
        sem = self.s_dbg
        self.dma(self.SP, o, ap, sem, reads=rd)


def build(nt=NT_FULL, debug=(), stop=None):
    P = Prog(nt, debug)
    nc = P.nc
    PE, ACT, DVE, POOL, SP = P.PE, P.ACT, P.DVE, P.POOL, P.SP
    R = P.R

    x_d = nc.dram_tensor("x", [TOK, D], F32, kind="ExternalInput").ap()
    p_d = nc.dram_tensor("p", [2, TOK, PLE], F32, kind="ExternalInput").ap()
    awin_d = nc.dram_tensor("a_w_in", [D, NA_IN], F32, kind="ExternalInput").ap()
    awout_d = nc.dram_tensor("a_w_out", [E, D], F32, kind="ExternalInput").ap()
    bwin_d = nc.dram_tensor("b_w_in", [D, NB_IN], F32, kind="ExternalInput").ap()
    bwout_d = nc.dram_tensor("b_w_out", [E, D], F32, kind="ExternalInput").ap()
    plew_d = nc.dram_tensor("ple_w", [2 * PLE, D], F32, kind="ExternalInput").ap()
    pleg_d = nc.dram_tensor("ple_gate_w", [2 * D, D], F32, kind="ExternalInput").ap()
    bd_d = nc.dram_tensor("bd", [128, 3 * NEC * 128], F32, kind="ExternalInput").ap()
    par_d = nc.dram_tensor("params", [128, NPAR], F32, kind="ExternalInput").ap()
    out_d = nc.dram_tensor("out", [TOK, D], F32, kind="ExternalOutput").ap()
    awin_b = nc.dram_tensor("awin_b", [D, NA_IN], BF16, kind="Internal").ap()
    awout_b = nc.dram_tensor("awout_b", [E, D], BF16, kind="Internal").ap()
    bwin_b = nc.dram_tensor("bwin_b", [D, NB_IN], BF16, kind="Internal").ap()
    bwout_b = nc.dram_tensor("bwout_b", [E, D], BF16, kind="Internal").ap()
    plew_b = nc.dram_tensor("plew_b", [2 * PLE, D], BF16, kind="Internal").ap()
    pleg_b = nc.dram_tensor("pleg_b", [2 * D, D], BF16, kind="Internal").ap()

    s_const = nc.alloc_semaphore("s_const")
    s_x = nc.alloc_semaphore("s_x")
    s_p = [nc.alloc_semaphore("s_p0"), nc.alloc_semaphore("s_p1")]
    s_o = nc.alloc_semaphore("s_o")
    P.s_dbg = nc.alloc_semaphore("s_dbg")
    s_ring = [nc.alloc_semaphore(f"s_ring{i}") for i in range(NSLOT)]

    xres, _, _ = P.sb("xres", [128, NDC, TT], F32)
    xT, _, _ = P.sb("xT", [128, NDC, TT], BF16)
    ring = []
    for i in range(NSLOT):
        t, _, _ = P.sb(f"ring{i}", [128, SLOT_BYTES // 2], BF16)
        ring.append(t)
    yT, yT_off, _ = P.sb("yT", [128, NEC, TT], BF16)
    x_in, _, _ = P.sb("x_in", [128, NCH, D], F32, at=yT_off)
    head_off = P.sb_off
    xmT, _, _ = P.sb("xmT", [128, 4, TT + 4], BF16)
    xcT, _, _ = P.sb("xcT", [128, 4, TT], BF16)
    qT, _, _ = P.sb("qT", [128, 4, TT], BF16)
    kT, _, _ = P.sb("kT", [128, 4, TT], BF16)
    ktok, ktok_off, _ = P.sb("ktok", [128, NCH, DH], BF16)
    vtok, vtok_off, _ = P.sb("vtok", [128, NCH, DH], BF16)
    hnT, _, _ = P.sb("hnT", [128, 4, TT], BF16)
    sigo, _, _ = P.sb("sigo", [128, 4, TT], BF16)
    siluz, _, _ = P.sb("siluz", [128, 4, TT], BF16)
    hn_tok, _, _ = P.sb("hn_tok", [128, DH], BF16)
    PT, _, _ = P.sb("PT", [128, 128], BF16)
    t1, _, _ = P.sb("t1", [128, TT], F32)
    t2, _, _ = P.sb("t2", [128, TT], F32)
    head_end = P.sb_off
    ostage, _, _ = P.sb("ostage", [128, NCH, D], F32, at=head_off)
    l1_off = head_off + NCH * D * 4
    uin, _, nb_ = P.sb("uin", [128, 4, TT + 4], BF16, at=l1_off)
    l1_off += nb_
    cgs, _, nb_ = P.sb("cgs", [128, TT], F32, at=l1_off)
    l1_off += nb_
    bgs, _, nb_ = P.sb("bgs", [128, TT], BF16, at=l1_off)
    l1_off += nb_
    szs, _, nb_ = P.sb("szs", [128, TT], BF16, at=l1_off)
    l1_off += nb_
    bz, _, nb_ = P.sb("bz", [128, TT], BF16, at=l1_off)
    l1_off += nb_
    assert l1_off <= head_end
    CT, _, _ = P.sb("CT", [128, H, 4, DH], F32)
    CTb, _, _ = P.sb("CTb", [128, H, 4, DH], BF16)
    nst, _, _ = P.sb("nst", [128, H, 4], F32)
    nbb, _, _ = P.sb("nbb", [128, H, 4], BF16)
    bd, _, _ = P.sb("bd_sb", [128, 3, NEC, 128], BF16)
    diag, _, _ = P.sb("diag", [128, 4, 4, 128], BF16)
    g_off = P.sb_off
    gi, _, _ = P.sb("gi", [4, TT], F32)
    gL, _, _ = P.sb("gL", [4, TT], F32)
    gNa, _, _ = P.sb("gNa", [4, TT], F32)
    gB, _, _ = P.sb("gB", [4, TT], F32)
    gG, _, _ = P.sb("gG", [4, TT], F32)
    gu, _, _ = P.sb("gu", [4, TT], F32)
    gfl, _, _ = P.sb("gfl", [4, TT], F32)
    g_end = P.sb_off
    o_ = g_off
    meanb, _, nb_ = P.sb("meanb", [128, TT], F32, at=o_)
    o_ += nb_
    rstdb, _, nb_ = P.sb("rstdb", [128, TT], F32, at=o_)
    o_ += nb_
    nbias, _, nb_ = P.sb("nbias", [128, TT], F32, at=o_)
    o_ += nb_
    sq0, _, nb_ = P.sb("sq0", [128, TT], F32, at=o_)
    o_ += nb_
    sq1, _, nb_ = P.sb("sq1", [128, TT], F32, at=o_)
    o_ += nb_
    gsb0, _, nb_ = P.sb("gsb0", [128, TT], F32, at=o_)
    o_ += nb_
    gsb1, _, nb_ = P.sb("gsb1", [128, TT], F32, at=o_)
    o_ += nb_
    assert o_ <= g_end
    sq = [sq0, sq1]
    gsb = [gsb0, gsb1]
    ident_f, _, _ = P.sb("ident_f", [128, 128], F32)
    ident_b, _, _ = P.sb("ident_b", [128, 128], BF16)
    mask_b, _, _ = P.sb("mask_b", [128, 128], BF16)
    onesD, _, _ = P.sb("onesD", [128, 128], F32)
    ones_f, _, _ = P.sb("ones_f", [128, 128], F32)
    par, _, _ = P.sb("par", [128, NPAR], F32)
    wg, _, _ = P.sb("wg", [128, NDC, 8], BF16)
    pstage0, _, _ = P.sb("pstage0", [128, NCH, PLE], F32)
    pstage1, _, _ = P.sb("pstage1", [128, NCH, PLE], F32)
    pstages = [pstage0, pstage1]
    pT, _, _ = P.sb("pT", [128, 2, TT], BF16)
    uf_tok, _, _ = P.sb("uf_tok", [128, NCH, 2, 4], F32)
    u_bf, _, _ = P.sb("u_bf", [128, NCH, 4], BF16)
    csb, _, _ = P.sb("csb", [128, NCH * 4], F32)
    kscale, _, _ = P.sb("kscale", [128, NCH * 4], F32)
    hist, _, _ = P.sb("hist", [128, NEC, 4], BF16)
    hist1, _, _ = P.sb("hist1", [128, NEC, 2], BF16)
    sm, _, _ = P.sb("sm", [128, 32], F32)
    bst, _, _ = P.sb("bst", [128, 12], F32)
    g4, _, _ = P.sb("g4", [4, 64], F32)
    rexp, _, _ = P.sb("rexp", [4, NCH, 4], F32)
    t2b, _, _ = P.sb("t2b", [128, TT], F32)
    print("SBUF used", P.sb_off - nc.sbuf_base, "of", P.sb_top - nc.sbuf_base)

    ps = [nc.alloc_psum_tensor(f"ps{i}", [128, 512], F32) for i in range(7)]
    psb = nc.alloc_psum_tensor("psb", [128, 1024], BF16)
    MAIN = [0, 1]
    UPD = [5, 6]
    MAIN_WIDE = [0, 1, 5, 6, 4]
    main_set = [MAIN]
    PS_S = 2
    PS_NS = [3, 4]
    PS_D = 2
    main_i = [0]
    upd_i = [0]

    def next_main():
        ms = main_set[0]
        i = ms[main_i[0] % len(ms)]
        main_i[0] += 1
        return ps[i], R("ps", i)

    def next_upd():
        i = UPD[upd_i[0] % len(UPD)]
        upd_i[0] += 1
        return ps[i], R("ps", i)

    def mm(out, lhsT, rhs, start, stop, reads, writes, inc):
        return P.op(PE, lambda e: e.matmul(out, lhsT=lhsT, rhs=rhs, start=start, stop=stop),
                    reads=reads, writes=writes, inc=inc)

    def tr(out, in_, ident, reads, writes, inc):
        return P.op(PE, lambda e: e.transpose(out=out, in_=in_, identity=ident),
                    reads=reads, writes=writes, inc=inc)

    def act(out, in_, func, reads, writes, bias=None, scale=None, E=None):
        kw = {}
        if bias is not None:
            kw["bias"] = bias
        if scale is not None:
            kw["scale"] = scale
        return P.op(ACT, lambda e: e.activation(out=out, in_=in_, func=func, **kw), reads=reads, writes=writes)

    def tt(E, out, in0, in1, op, reads, writes):
        return P.op(E, lambda e: e.tensor_tensor(out=out, in0=in0, in1=in1, op=op), reads=reads, writes=writes)

    def ts(E, out, in0, s1, op0, reads, writes, s2=None, op1=None):
        if op1 is None:
            return P.op(E, lambda e: e.tensor_scalar(out=out, in0=in0, scalar1=s1, scalar2=None, op0=op0),
                        reads=reads, writes=writes)
        return P.op(E, lambda e: e.tensor_scalar(out=out, in0=in0, scalar1=s1, scalar2=s2, op0=op0, op1=op1),
                    reads=reads, writes=writes)

    def stt(E, out, in0, scalar, in1, op0, op1, reads, writes):
        return P.op(E, lambda e: e.scalar_tensor_tensor(out=out, in0=in0, scalar=scalar, in1=in1, op0=op0, op1=op1),
                    reads=reads, writes=writes)

    def cp(E, out, in_, reads, writes):
        return P.op(E, lambda e: e.tensor_copy(out=out, in_=in_), reads=reads, writes=writes)

    def memset(E, ap, val, writes):
        return P.op(E, lambda e: e.memset(ap, val), writes=writes)

    r_const = R("const")
    P.dma(SP, par[:, :], par_d[:, :], s_const, writes=[r_const])
    s_bd = nc.alloc_semaphore("s_bd")
    s_wg = nc.alloc_semaphore("s_wg")
    r_bd = R("bd_w")
    r_wg = R("wg_w")
    P.dma(POOL, bd[:, :, :, :], bd_d.rearrange("p (m c n) -> p m c n", m=3, c=NEC), s_bd, writes=[r_bd])
    P.dma(POOL, wg[:, :, :], awin_d[:, 3 * E:3 * E + 8].rearrange("(c p) n -> p c n", p=128), s_wg, writes=[r_wg])

    r_id = R("ident")
    memset(POOL, ident_f[:, :], 0.0, [r_id])
    P.op(POOL, lambda e: e.affine_select(out=ident_f[:, :], in_=ident_f[:, :], pattern=[[-1, 128]],
                                         compare_op=ALU.not_equal, fill=1.0, base=0, channel_multiplier=1),
         reads=[r_id], writes=[r_id])
    cp(POOL, ident_b[:, :], ident_f[:, :], [r_id], [R("ident_b")])
    r_mask = R("mask")
    memset(POOL, onesD[:, :], 1.0, [R("onesD")])
    P.op(POOL, lambda e: e.affine_select(out=mask_b[:, :], in_=onesD[:, :], pattern=[[1, 128]],
                                         compare_op=ALU.is_ge, fill=0.0, base=0, channel_multiplier=-1),
         reads=[R("onesD")], writes=[r_mask])
    memset(POOL, ones_f[:, :], 1.0, [R("ones_f")])
    memset(POOL, onesD[:, :], 1.0 / D, [R("onesD")])
    memset(DVE, sm[:, 16:17], -0.5, [R("nhalf")])
    ts(DVE, par[:, O_GNH:O_GNH + 16], par[:, O_GN:O_GN + 16], 0.5, ALU.mult, [r_const], [R("gnh")])
    ts(DVE, par[0:4, O_BF:O_BF + 1], par[0:4, O_BF:O_BF + 1], -1.0, ALU.mult, [r_const], [R("negbf")])

    plan = []
    for h in range(H):
        plan.append(("a_xm", h, awin_d[:, h * DH:(h + 1) * DH], awin_b[:, h * DH:(h + 1) * DH], NDC, 512))
        plan.append(("a_o", h, awin_d[:, E + h * DH:E + (h + 1) * DH], awin_b[:, E + h * DH:E + (h + 1) * DH], NDC, 512))
        plan.append(("a_z", h, awin_d[:, 2 * E + h * DH:2 * E + (h + 1) * DH], awin_b[:, 2 * E + h * DH:2 * E + (h + 1) * DH], NDC, 512))

    def plan_tail(l, wout_b, wout_d):
        for dg in range(4):
            plan.append((f"out{l}", dg, wout_d[:, dg * 256:(dg + 1) * 256], wout_b[:, dg * 256:(dg + 1) * 256], NEC, 256))
        for hf in range(2):
            plan.append((f"pg{l}", hf, pleg_d[l * D:(l + 1) * D, hf * 512:(hf + 1) * 512], pleg_b[l * D:(l + 1) * D, hf * 512:(hf + 1) * 512], NDC, 512))
            plan.append((f"pw{l}", hf, plew_d[l * PLE:(l + 1) * PLE, hf * 512:(hf + 1) * 512], plew_b[l * PLE:(l + 1) * PLE, hf * 512:(hf + 1) * 512], 2, 512))

    plan_tail(0, awout_b, awout_d)
    for g in range(4):
        for nm, j in (("b_cg", 1), ("b_hx", 2), ("b_bg", 0), ("b_z", 3)):
            plan.append((nm, g, bwin_d[:, j * E + g * DH:j * E + (g + 1) * DH], bwin_b[:, j * E + g * DH:j * E + (g + 1) * DH], NDC, 512))
    plan_tail(1, bwout_b, bwout_d)
    nplan = len(plan)
    memset(POOL, CT[:, :, :, :], 0.0, [R("CT", h, k) for h in range(H) for k in range(4)])
    memset(POOL, CTb[:, :, :, :], 0.0, [R("CTb", h, k) for h in range(H) for k in range(4)])
    scr = []
    for (nm, idx, fsrc, bsrc, kc, ncol) in plan:
        rows = kc * 128
        step = rows if rows <= 1024 else rows // 2
        rl = []
        for j, r0 in enumerate(range(0, rows, step)):
            sem = nc.alloc_semaphore(f"s_prep_{nm}_{idx}_{j}")
            POOL.eng.dma_start(out=bsrc[r0:r0 + step, :], in_=fsrc[r0:r0 + step, :]).then_inc(sem, 16)
            P.dcnt[sem] = 16
            r = R("scr", nm, idx, j)
            r.w = (sem, 16)
            rl.append(r)
        scr.append(rl)
    total_uses = nplan * nt
    wstate = {"issued": 0, "next": 0, "rel": 0}

    def ring_issue():
        while wstate["issued"] < min(wstate["rel"] + NSLOT, total_uses):
            m = wstate["issued"]
            _, _, _, src, kc, ncol = plan[m % nplan]
            sl = m % NSLOT
            dst = ring[sl][:, 0:kc * ncol].rearrange("p (c n) -> p c n", c=kc)
            P.dma(SP, dst, src.rearrange("(c p) n -> p c n", p=128), s_ring[sl], reads=scr[m % nplan], writes=[R("ring", sl)])
            wstate["issued"] += 1

    def ring_get(name, idx):
        n = wstate["next"]
        key = plan[n % nplan]
        assert key[0] == name and key[1] == idx, (key[0], key[1], name, idx)
        ring_issue()
        assert wstate["issued"] > n, "ring deadlock"
        wstate["next"] = n + 1
        sl = n % NSLOT
        _, _, _, _, kc, ncol = key
        return ring[sl][:, 0:kc * ncol].rearrange("p (c n) -> p c n", c=kc), R("ring", sl)

    def ring_rel(k=1):
        wstate["rel"] += k
        assert wstate["rel"] <= wstate["next"]
        ring_issue()

    r_xres = P.Rs("xres", NDC)
    r_xT = P.Rs("xT", NDC)
    r_yT = P.Rs("yT", NEC)
    r_xin = R("x_in")
    r_ost = R("ostage")
    HEADBUFS = ["xmT", "xcT", "qT", "kT", "ktok", "vtok", "hnT", "sigo", "siluz"]

    def head_res():
        out = []
        for nm in HEADBUFS:
            out += P.Rs(nm, 4)
        out += [R("hn_tok"), R("PT"), R("t1"), R("t2", 0)]
        return out

    def l1_res():
        return P.Rs("uin", 4) + [R("cgs"), R("bgs"), R("szs"), R("bz")]

    def gate_res():
        return [R(n) for n in ("gi", "gL", "gNa", "gB", "gG", "gu", "gfl")]

    def ln_res2():
        return [R("meanb"), R("rstdb"), R("nbias"), R("sq", 0), R("sq", 1), R("gsb", 0), R("gsb", 1)]

    r_par = r_const
    evac_rr = [0]

    def evac_copy(out, in_, reads, writes, scale=None):
        evac_rr[0] += 1
        if scale is not None or evac_rr[0] % 2 == 0:
            return act(out, in_, AF.Copy, reads, writes, scale=scale)
        return cp(DVE, out, in_, reads, writes)

    def load_x(it):
        src = x_d[it * TT:(it + 1) * TT, :].rearrange("(c p) d -> p c d", p=128)
        P.transfer(r_yT, [r_xin])
        P.dma(SP, x_in[:, :, :], src, s_x, writes=[r_xin])

    def x_transposes():
        for dc in range(NDC):
            pt, rp = next_main()
            for c in range(NCH):
                tr(pt[:, c * 128:(c + 1) * 128], x_in[:, c, dc * 128:(dc + 1) * 128], ident_f[:, :],
                   [r_xin, r_id], [rp], inc=(c == NCH - 1))
            cp(DVE, xres[:, dc, :], pt[:, :], [rp], [r_xres[dc]])
            act(xT[:, dc, :], pt[:, :], AF.Copy, [rp], [r_xT[dc]])
        P.transfer([r_xin], r_yT)

    def issue_p(l, it):
        src = p_d[l, it * TT:(it + 1) * TT, :].rearrange("(c p) f -> p c f", p=128)
        P.dma(SP, pstages[l][:, :, :], src, s_p[l], writes=[R("pstage", l)])

    def load_p(l, it):
        pstage = pstages[l]
        for pc in range(2):
            pt, rp = next_main()
            for c in range(NCH):
                tr(pt[:, c * 128:(c + 1) * 128], pstage[:, c, pc * 128:(pc + 1) * 128], ident_f[:, :],
                   [R("pstage", l), r_id], [rp], inc=(c == NCH - 1))
            act(pT[:, pc, :], pt[:, :], AF.Copy, [rp], [R("pT", pc)])

    def reset_state(first=False):
        if not first:
            memset(POOL, CT[:, :, :, :], 0.0, [R("CT", h, k) for h in range(H) for k in range(4)])
            memset(POOL, CTb[:, :, :, :], 0.0, [R("CTb", h, k) for h in range(H) for k in range(4)])
        memset(DVE, nst[:, :, :], 0.0, [R("nst", h) for h in range(H)])
        memset(DVE, nbb[:, :, :], 0.0, [R("nbb", h) for h in range(H)])
        memset(DVE, hist[:, :, :], 0.0, P.Rs("hist", H))
        memset(DVE, hist1[:, :, :], 0.0, P.Rs("hist1", 4))
        memset(DVE, g4[:, 0:2], 0.0, [R("carry")])

    def gates_a():
        P.transfer(ln_res2(), gate_res())
        pi, rpi = next_main()
        pf, rpf = next_main()
        for dc in range(NDC):
            mm(pi[0:4, :], wg[:, dc, 0:4], xT[:, dc, :], dc == 0, dc == NDC - 1, [r_wg, r_xT[dc]], [rpi], inc=(dc == NDC - 1))
        for dc in range(NDC):
            mm(pf[0:4, :], wg[:, dc, 4:8], xT[:, dc, :], dc == 0, dc == NDC - 1, [r_wg, r_xT[dc]], [rpf], inc=(dc == NDC - 1))
        act(gi[:, :], pi[0:4, :], AF.Identity, [rpi, r_par], [R("gi")], bias=par[0:4, O_BI:O_BI + 1])
        act(gL[:, :], pf[0:4, :], AF.Exp, [rpf, R("negbf")], [R("gL")], bias=par[0:4, O_BF:O_BF + 1], scale=-1.0)
        act(gL[:, :], gL[:, :], AF.Ln, [R("gL")], [R("gL")], bias=1.0)
        rc = R("carry")
        P.op(DVE, lambda e: e.tensor_tensor_scan(out=gNa[:, :], data0=onesrow, data1=gL[:, :],
                                                 initial=g4[:, 0:1], op0=ALU.mult, op1=ALU.add),
             reads=[R("gL"), rc, R("ones_f")], writes=[R("gNa")])
        tt(DVE, gB[:, :], gi[:, :], gNa[:, :], ALU.add, [R("gi"), R("gNa")], [R("gB")])
        P.op(DVE, lambda e: e.tensor_tensor_scan(out=gG[:, :], data0=gB[:, :], data1=gB[:, :],
                                                 initial=g4[:, 1:2], op0=ALU.max, op1=ALU.max),
             reads=[R("gB"), rc], writes=[R("gG")])
        rg = R("g4s")
        cp(DVE, g4[:, 8:9], g4[:, 1:2], [rc], [rg])
        for c in range(1, NCH):
            cp(DVE, g4[:, 8 + c:9 + c], gG[:, c * 128 - 1:c * 128], [R("gG")], [rg])
        for c in range(NCH):
            cp(DVE, g4[:, 12 + c:13 + c], gG[:, c * 128 + 127:c * 128 + 128], [R("gG")], [rg])
        ts(DVE, g4[:, 16:20], g4[:, 8:12], -1.0, ALU.mult, [rg], [rg])
        tt(DVE, g4[:, 24:28], g4[:, 8:12], g4[:, 12:16], ALU.subtract, [rg], [rg])
        act(g4[:, 20:24], g4[:, 24:28], AF.Exp, [rg], [rg])
        for c in range(NCH):
            sl = slice(c * 128, (c + 1) * 128)
            act(gu[:, sl], gB[:, sl], AF.Exp, [R("gB"), rg], [R("gu")], bias=g4[:, 16 + c:17 + c])
            act(gfl[:, sl], gNa[:, sl], AF.Exp, [R("gNa"), rg], [R("gfl")], bias=g4[:, 16 + c:17 + c])
        cp(DVE, g4[:, 0:1], gNa[:, TT - 1:TT], [R("gNa")], [rc])
        cp(DVE, g4[:, 1:2], gG[:, TT - 1:TT], [R("gG")], [rc])
    def gates_b():
        rg = R("g4s")
        pt, rp = next_main()
        for c in range(NCH):
            sl = slice(c * 128, (c + 1) * 128)
            tr(pt[:, c * 8:c * 8 + 4], gu[:, sl], ident_f[0:4, 0:4], [R("gu"), r_id], [rp], inc=False)
            tr(pt[:, c * 8 + 4:c * 8 + 8], gfl[:, sl], ident_f[0:4, 0:4], [R("gfl"), r_id], [rp], inc=(c == NCH - 1))
        cp(DVE, uf_tok[:, :, :, :], pt[:, 0:NCH * 8].rearrange("p (c t h) -> p c t h", c=NCH, t=2), [rp], [R("uf_tok")])
        cp(DVE, u_bf[:, :, :], uf_tok[:, :, 0, :], [R("uf_tok")], [R("u_bf")])
        for c in range(NCH):
            ts(DVE, rexp[:, c, :], ident_f[0:4, 0:4], g4[:, 20 + c:21 + c], ALU.mult, [rg, r_id], [R("rexp")])
        pt2, rp2 = next_main()
        mm(pt2[:, 0:NCH * 4], ones_f[0:4, :], rexp[:, :, :].rearrange("p c h -> p (c h)"), True, True,
           [R("ones_f"), R("rexp")], [rp2], inc=True)
        cp(DVE, csb[:, :], pt2[:, 0:NCH * 4], [rp2], [R("csb")])
        ts(DVE, kscale[:, :], csb[:, :], KSC, ALU.mult, [R("csb")], [R("kscale")])
        P.dump("gu", gu[:, :], [4, TT], rd=[R("gu")])
        P.dump("gfl", gfl[:, :], [4, TT], rd=[R("gfl")])
        P.dump("csb", csb[:, :], [128, 16], rd=[R("csb")])
        P.dump("uf_tok", uf_tok[:, :, :, :], [128, NCH, 2, 4], rd=[R("uf_tok")])

    onesrow = None
    cur = {"it": 0}

    def proj_group(w, rw, dst_fn):
        for ec in range(4):
            pt, rp = next_main()
            for dc in range(NDC):
                mm(pt[:, :], w[:, dc, ec * 128:(ec + 1) * 128], xT[:, dc, :], dc == 0, dc == NDC - 1,
                   [rw, r_xT[dc]], [rp], inc=(dc == NDC - 1))
            dst_fn(ec, pt, rp)

    def head_A(h):
        w, rw = ring_get("a_xm", h)
        rxm = P.Rs("xmT", 4)
        rxc = P.Rs("xcT", 4)
        for ec in range(4):
            cp(DVE, xmT[:, ec, 0:3], hist[:, 4 * h + ec, 0:3], [R("hist", h)], [rxm[ec]])

        def ev_xm(ec, pt, rp):
            act(xmT[:, ec, 3:3 + TT], pt[:, :], AF.Copy, [rp], [rxm[ec]])
        proj_group(w, rw, ev_xm)
        ring_rel()
        for ec in range(4):
            cp(DVE, hist[:, 4 * h + ec, 0:3], xmT[:, ec, TT:TT + 3], [rxm[ec]], [R("hist", h)])

    def build_diag(h):
        rdg = R("diag")
        for ec in range(4):
            for k in range(4):
                c0 = O_CWA + (4 * h + ec) * 4 + k
                ts(DVE, diag[:, k, ec, :], ident_b[:, :], par[:, c0:c0 + 1], ALU.mult, [R("ident_b"), r_par], [rdg])

    def head_A1b(h):
        rxm = P.Rs("xmT", 4)
        rxc = P.Rs("xcT", 4)
        rdg = R("diag")
        for ec in range(4):
            pt, rp = next_main()
            for k in range(4):
                mm(pt[:, :], diag[:, k, ec, :], xmT[:, ec, k:k + TT], k == 0, k == 3, [rdg, rxm[ec]], [rp], inc=(k == 3))
            c0 = O_CBA + 4 * h + ec
            act(xcT[:, ec, :], pt[:, :], AF.Silu, [rp, r_par], [rxc[ec]], bias=par[:, c0:c0 + 1])
        for ec in range(4):
            pt, rp = next_main()
            mm(pt[:, :], bd[:, 0, 4 * h + ec, :], xcT[:, ec, :], True, True, [r_bd, rxc[ec]], [rp], inc=True)
            evac_copy(qT[:, ec, :], pt[:, :], [rp], [R("qT", ec)])
            pt, rp = next_main()
            mm(pt[:, :], bd[:, 1, 4 * h + ec, :], xcT[:, ec, :], True, True, [r_bd, rxc[ec]], [rp], inc=True)
            act(kT[:, ec, :], pt[:, :], AF.Copy, [rp], [R("kT", ec)], scale=KSC)
    def head_A2(h):
        rxm = P.Rs("xmT", 4)
        rxc = P.Rs("xcT", 4)
        for c in range(NCH):
            pt, rp = next_main()
            for ec in range(4):
                mm(pt[:, ec * 128:(ec + 1) * 128], xcT[:, ec, c * 128:(c + 1) * 128], bd[:, 1, 4 * h + ec, :], True, True,
                   [r_bd, rxc[ec]], [rp], inc=(ec == 3))
            act(ktok[:, c, :], pt[:, :], AF.Copy, [rp, R("kscale")], [R("ktok", c)], scale=kscale[:, c * 4 + h:c * 4 + h + 1])
            pt, rp = next_main()
            for ec in range(4):
                mm(pt[:, ec * 128:(ec + 1) * 128], xmT[:, ec, 3 + c * 128:3 + (c + 1) * 128], bd[:, 2, 4 * h + ec, :], True, True,
                   [r_bd, rxm[ec]], [rp], inc=(ec == 3))
            act(vtok[:, c, :], pt[:, :], AF.Copy, [rp, R("uf_tok")], [R("vtok", c)], scale=uf_tok[:, c, 0, h:h + 1])
        if h == 0:
            P.dump("xmT", xmT[:, :, :], [128, 4, TT + 4], BF16, rd=rxm)
            P.dump("xcT", xcT[:, :, :], [128, 4, TT], BF16, rd=rxc)
            P.dump("qT", qT[:, :, :], [128, 4, TT], BF16, rd=P.Rs("qT", 4))
            P.dump("kT", kT[:, :, :], [128, 4, TT], BF16, rd=P.Rs("kT", 4))
            P.dump("ktok", ktok[:, :, :], [128, NCH, DH], BF16, rd=P.Rs("ktok", 4))
            P.dump("vtok", vtok[:, :, :], [128, NCH, DH], BF16, rd=P.Rs("vtok", 4))

    def mlstm_part1(h, c):
        sl = slice(c * 128, (c + 1) * 128)
        par_ = c % 2
        rq = P.Rs("qT", 4)
        rk = P.Rs("kT", 4)
        pS = ps[PS_S]
        pD = ps[PS_D]
        rS, rD = R("ps", PS_S), R("ps", PS_D)
        rUn = rD
        pN, rN = ps[PS_NS[par_]], R("ps", PS_NS[par_])
        rs = R("sm", par_)
        o = 8 * par_
        for dk in range(4):
            mm(pS[:, 0:128], kT[:, dk, sl], qT[:, dk, sl], dk == 0, dk == 3, [rk[dk], rq[dk]], [rS], inc=(dk == 3))
        tt(DVE, PT[:, :], pS[:, 0:128], mask_b[:, :], ALU.mult, [rS, r_mask], [R("PT")])
        cs_col = csb[:, c * 4 + h:c * 4 + h + 1]
        upd_banks = []
        for dk in range(4):
            pu, ru = next_upd()
            upd_banks.append((pu, ru))
            mm(pu[:, :], ktok[:, c, dk * 128:(dk + 1) * 128], vtok[:, c, :], True, True, [R("ktok", c), R("vtok", c)], [ru], inc=True)
            if dk % 2 == 1:
                for d2 in (dk - 1, dk):
                    pu2, ru2 = upd_banks[d2]
                    stt(DVE, CT[:, h, d2, :], CT[:, h, d2, :], cs_col, pu2[:, :], ALU.mult, ALU.add, [R("CT", h, d2), R("csb"), ru2], [R("CT", h, d2)])
        mm(pN[:, :], PT[:, :], vtok[:, c, :], True, False, [R("PT"), R("vtok", c)], [rN], inc=False)
        for dk in range(4):
            mm(pN[:, :], qT[:, dk, sl], CTb[:, h, dk, :], False, dk == 3, [rq[dk], R("CTb", h, dk)], [rN], inc=(dk == 3))
        mm(pD[:, 256:257], PT[:, :], u_bf[:, c, h:h + 1], True, False, [R("PT"), R("u_bf")], [rD], inc=False)
        for dk in range(4):
            mm(pD[:, 256:257], qT[:, dk, sl], nbb[:, h, dk:dk + 1], False, dk == 3, [rq[dk], R("nbb", h)], [rD], inc=(dk == 3))
        cp(DVE, sm[:, o + 7:o + 8], pD[:, 256:257], [rD], [rs])
        for dk in range(4):
            act(CTb[:, h, dk, :], CT[:, h, dk, :], AF.Copy, [R("CT", h, dk)], [R("CTb", h, dk)])
        for dk in range(4):
            mm(pD[:, 260 + dk:261 + dk], ktok[:, c, dk * 128:(dk + 1) * 128], u_bf[:, c, h:h + 1], True, True,
               [R("ktok", c), R("u_bf")], [rUn], inc=(dk == 3))
        stt(DVE, nst[:, h, :], nst[:, h, :], cs_col, pD[:, 260:264], ALU.mult, ALU.add, [R("nst", h), R("csb"), rUn], [R("nst", h)])
        cp(DVE, nbb[:, h, :], nst[:, h, :], [R("nst", h)], [R("nbb", h)])

    def mlstm_part2(h, c):
        sl = slice(c * 128, (c + 1) * 128)
        par_ = c % 2
        pN, rN = ps[PS_NS[par_]], R("ps", PS_NS[par_])
        rs = R("sm", par_)
        o = 8 * par_
        rbst = R("bst", par_)
        bs = bst[:, 6 * par_:6 * par_ + 6]
        stt(DVE, sm[:, o:o + 1], sm[:, o + 7:o + 8], -1.0, sm[:, o + 7:o + 8], ALU.mult, ALU.max, [rs], [rs])
        tt(DVE, sm[:, o:o + 1], sm[:, o:o + 1], uf_tok[:, c, 1, h:h + 1], ALU.max, [rs, R("uf_tok")], [rs])
        P.op(DVE, lambda e: e.bn_stats(out=bs, in_=pN[:, :]), reads=[rN], writes=[rbst])
        P.op(DVE, lambda e: e.bn_aggr(out=sm[:, o + 2:o + 4], in_=bs), reads=[rbst], writes=[rs])
        tt(DVE, sm[:, o + 1:o + 2], sm[:, o:o + 1], sm[:, o:o + 1], ALU.mult, [rs], [rs])
        stt(DVE, sm[:, o + 4:o + 5], sm[:, o + 1:o + 2], GN_EPS, sm[:, o + 3:o + 4], ALU.mult, ALU.add, [rs], [rs])
        if cur["it"] == 0:
            act(sm[:, o + 5:o + 6], sm[:, o + 4:o + 5], AF.Sqrt, [rs], [rs])
            P.op(DVE, lambda e: e.reciprocal(out=sm[:, o + 6:o + 7], in_=sm[:, o + 5:o + 6]), reads=[rs], writes=[rs])
        else:
            tt(POOL, sm[:, o + 6:o + 7], sm[:, o + 4:o + 5], sm[:, 16:17], ALU.pow, [rs, R("nhalf")], [rs])
        ts(DVE, hn_tok[:, :], pN[:, :], sm[:, o + 2:o + 3], ALU.subtract, [rN, rs], [R("hn_tok")], s2=sm[:, o + 6:o + 7], op1=ALU.mult)

    def mlstm_part2b(h, c):
        sl = slice(c * 128, (c + 1) * 128)
        rb = R("psb", 0)
        for ec in range(4):
            tr(psb[:, ec * 128:(ec + 1) * 128], hn_tok[:, ec * 128:(ec + 1) * 128], ident_b[:, :], [R("hn_tok"), R("ident_b")], [rb], inc=(ec == 3))
        for ec in range(4):
            e = 4 * h + ec
            act(hnT[:, ec, sl], psb[:, ec * 128:(ec + 1) * 128], AF.Copy, [rb, R("gnh")], [R("hnT", ec)], scale=par[:, O_GNH + e:O_GNH + e + 1])

    def head_C_proj(h, which, ecs):
        w, rw = which
        for ec in ecs:
            pt, rp = next_main()
            for dc in range(NDC):
                mm(pt[:, :], w[:, dc, ec * 128:(ec + 1) * 128], xT[:, dc, :], dc == 0, dc == NDC - 1, [rw, r_xT[dc]], [rp], inc=(dc == NDC - 1))
            yield ec, pt, rp

    def head_BC(h):
        wo = ring_get("a_o", h)
        wz = ring_get("a_z", h)
        pieces = [(wo, [0, 1], True), (wo, [2, 3], True), (wz, [0, 1], False), (wz, [2, 3], False)]
        def piece(c):
            wv, ecs, is_o = pieces[c]
            for ec, pt, rp in head_C_proj(h, wv, ecs):
                if is_o:
                    act(sigo[:, ec, :], pt[:, :], AF.Tanh, [rp], [R("sigo", ec)], scale=0.5)
                else:
                    act(siluz[:, ec, :], pt[:, :], AF.Silu, [rp], [R("siluz", ec)])
            if c % 2 == 1:
                ring_rel()
        mlstm_part1(h, 0)
        piece(0)
        mlstm_part1(h, 1)
        mlstm_part2(h, 0)
        piece(1)
        mlstm_part2b(h, 0)
        mlstm_part1(h, 2)
        mlstm_part2(h, 1)
        piece(2)
        mlstm_part2b(h, 1)
        mlstm_part1(h, 3)
        mlstm_part2(h, 2)
        piece(3)
        mlstm_part2b(h, 2)
        mlstm_part2(h, 3)

    def head_BC_fin(h):
        mlstm_part2b(h, 3)
        for ec in range(4):
            e = 4 * h + ec
            t2x = t2 if ec % 2 == 0 else t2b
            rt2 = R("t2", ec % 2)
            stt(DVE, t1[:, :], sigo[:, ec, :], 1.0, hnT[:, ec, :], ALU.add, ALU.mult, [R("hnT", ec), R("sigo", ec)], [R("t1")])
            stt(DVE, t2x[:, :], xcT[:, ec, :], par[:, O_SKIP + e:O_SKIP + e + 1], t1[:, :], ALU.mult, ALU.add,
                [R("xcT", ec), R("t1"), r_par], [rt2])
            tt(DVE if cur["it"] == 0 else POOL, yT[:, e, :], t2x[:, :], siluz[:, ec, :], ALU.mult, [rt2, R("siluz", ec)], [r_yT[e]])
        if h == 0:
            P.dump("hnT", hnT[:, :, :], [128, 4, TT], BF16, rd=P.Rs("hnT", 4))
            P.dump("sigo", sigo[:, :, :], [128, 4, TT], BF16, rd=P.Rs("sigo", 4))
            P.dump("siluz", siluz[:, :, :], [128, 4, TT], BF16, rd=P.Rs("siluz", 4))
            P.dump("t1", t1[:, :], [128, TT], F32, rd=[R("t1")])
            P.dump("yT0", yT[:, 0:4, :], [128, 4, TT], BF16, rd=r_yT[0:4])

    def tail(l, it):
        load_p(l, it)
        if l == 0:
            P.transfer(gate_res(), ln_res2())
        pm, rpm = ps[PS_S], R("ps", PS_S)
        pq, rpq = ps[PS_NS[0]], R("ps", PS_NS[0])

        def ln_stats(dc):
            b = dc % 2
            act(sq[b][:, :], xres[:, dc, :], AF.Square, [r_xres[dc]], [R("sq", b)])
            mm(pm[:, :], onesD[:, :], xres[:, dc, :], dc == 0, dc == NDC - 1, [R("onesD"), r_xres[dc]], [rpm], inc=(dc == NDC - 1))
            mm(pq[:, :], onesD[:, :], sq[b][:, :], dc == 0, dc == NDC - 1, [R("onesD"), R("sq", b)], [rpq], inc=True)

        pending = None
        for dg in range(4):
            w, rw = ring_get(f"out{l}", dg)
            for j in range(2):
                dc = 2 * dg + j
                pt, rp = next_main()
                for ec in range(NEC):
                    mm(pt[:, :], w[:, ec, j * 128:(j + 1) * 128], yT[:, ec, :], ec == 0, ec == NEC - 1, [rw, r_yT[ec]], [rp], inc=(ec == NEC - 1))
                stt(DVE, xres[:, dc, :], xres[:, dc, :], ALPHA, pt[:, :], ALU.mult, ALU.add, [r_xres[dc], rp], [r_xres[dc]])
                if pending is not None:
                    ln_stats(pending)
                pending = dc
            ring_rel()
        if l == 1 and it + 1 < nt:
            load_x(it + 1)
        ln_stats(pending)
        act(meanb[:, :], pm[:, :], AF.Copy, [rpm], [R("meanb")])
        tt(DVE, nbias[:, :], meanb[:, :], meanb[:, :], ALU.mult, [R("meanb")], [R("nbias")])
        stt(DVE, rstdb[:, :], pq[:, :], LN_EPS, nbias[:, :], ALU.add, ALU.subtract, [rpq, R("nbias")], [R("rstdb")])
        act(rstdb[:, :], rstdb[:, :], AF.Sqrt, [R("rstdb")], [R("rstdb")])
        P.op(DVE, lambda e: e.reciprocal(out=rstdb[:, :], in_=rstdb[:, :]), reads=[R("rstdb")], writes=[R("rstdb")])
        tt(DVE, nbias[:, :], meanb[:, :], rstdb[:, :], ALU.mult, [R("meanb"), R("rstdb")], [R("nbias")])
        for dc in range(NDC):
            tt(DVE, xres[:, dc, :], xres[:, dc, :], rstdb[:, :], ALU.mult, [r_xres[dc], R("rstdb")], [r_xres[dc]])
            tt(DVE, xres[:, dc, :], xres[:, dc, :], nbias[:, :], ALU.subtract, [r_xres[dc], R("nbias")], [r_xres[dc]])
            cg_ = O_LNG + l * 8 + dc
            cb_ = O_LNB + l * 8 + dc
            act(xT[:, dc, :], xres[:, dc, :], AF.Identity, [r_xres[dc], r_par], [r_xT[dc]], bias=par[:, cb_:cb_ + 1], scale=par[:, cg_:cg_ + 1])
            ts(POOL, xres[:, dc, :], xres[:, dc, :], par[:, cg_:cg_ + 1], ALU.mult, [r_xres[dc], r_par], [r_xres[dc]], s2=par[:, cb_:cb_ + 1], op1=ALU.add)
        if l == 0 and it == 0:
            P.dump("xln0", xres[:, :, :], [128, NDC, TT], rd=r_xres)
        for hf in range(2):
            wG, rwG = ring_get(f"pg{l}", hf)
            wP, rwP = ring_get(f"pw{l}", hf)
            for j in range(4):
                dc = 4 * hf + j
                b = j % 2
                pt, rp = next_main()
                for di in range(NDC):
                    mm(pt[:, :], wG[:, di, j * 128:(j + 1) * 128], xT[:, di, :], di == 0, di == NDC - 1, [rwG, r_xT[di]], [rp], inc=(di == NDC - 1))
                act(gsb[b][:, :], pt[:, :], AF.Tanh, [rp], [R("gsb", b)], scale=0.5)
                pt2, rp2 = next_main()
                for pc in range(2):
                    mm(pt2[:, :], wP[:, pc, j * 128:(j + 1) * 128], pT[:, pc, :], pc == 0, pc == 1, [rwP, R("pT", pc)], [rp2], inc=(pc == 1))
                stt(DVE, gsb[b][:, :], gsb[b][:, :], 1.0, pt2[:, :], ALU.add, ALU.mult, [R("gsb", b), rp2], [R("gsb", b)])
                stt(DVE, xres[:, dc, :], gsb[b][:, :], 0.5, xres[:, dc, :], ALU.mult, ALU.add, [r_xres[dc], R("gsb", b)], [r_xres[dc]])
            ring_rel(2)
        if l == 0:
            for dc in range(NDC):
                if dc % 2:
                    cp(DVE, xT[:, dc, :], xres[:, dc, :], [r_xres[dc]], [r_xT[dc]])
                else:
                    act(xT[:, dc, :], xres[:, dc, :], AF.Copy, [r_xres[dc]], [r_xT[dc]])
        if l == 0 and it == 0:
            P.dump("x1", xres[:, :, :], [128, NDC, TT], rd=r_xres)

    def layer1(it):
        P.transfer(head_res(), l1_res())
        ruin = P.Rs("uin", 4)
        for g in range(4):
            wcg = ring_get("b_cg", g)
            whx = ring_get("b_hx", g)
            for ec in range(4):
                cp(DVE, uin[:, ec, 0:2], hist1[:, 4 * g + ec, 0:2], [R("hist1", g)], [ruin[ec]])
            rdg = R("diag")
            for ec in range(4):
                for k in range(3):
                    c0 = O_CWB + (4 * g + ec) * 3 + k
                    ts(DVE, diag[:, k, ec, :], ident_b[:, :], par[:, c0:c0 + 1], ALU.mult, [R("ident_b"), r_par], [rdg])

            def mmproj(wv, ec):
                w, rw = wv
                pt, rp = next_main()
                for dc in range(NDC):
                    mm(pt[:, :], w[:, dc, ec * 128:(ec + 1) * 128], xT[:, dc, :], dc == 0, dc == NDC - 1, [rw, r_xT[dc]], [rp], inc=(dc == NDC - 1))
                return pt, rp

            for ec in range(4):
                pt, rp = mmproj(wcg, ec)
                act(cgs[:, :], pt[:, :], AF.Copy, [rp], [R("cgs")])
                pt, rp = mmproj(whx, ec)
                tt(DVE, uin[:, ec, 2:2 + TT], pt[:, :], cgs[:, :], ALU.mult, [rp, R("cgs")], [ruin[ec]])
            ring_rel(2)
            for ec in range(4):
                cp(DVE, hist1[:, 4 * g + ec, 0:2], uin[:, ec, TT:TT + 2], [ruin[ec]], [R("hist1", g)])
            wbg = ring_get("b_bg", g)
            wz = ring_get("b_z", g)
            for ec in range(4):
                e = 4 * g + ec
                pt, rp = mmproj(wbg, ec)
                act(bgs[:, :], pt[:, :], AF.Copy, [rp], [R("bgs")])
                pt, rp = mmproj(wz, ec)
                act(szs[:, :], pt[:, :], AF.Silu, [rp], [R("szs")])
                tt(POOL, bz[:, :], bgs[:, :], szs[:, :], ALU.mult, [R("bgs"), R("szs")], [R("bz")])
                pu, rpu = next_main()
                for k in range(3):
                    mm(pu[:, :], diag[:, k, ec, :], uin[:, ec, k:k + TT], k == 0, k == 2, [rdg, ruin[ec]], [rpu], inc=(k == 2))
                tt(DVE, yT[:, e, :], pu[:, :], bz[:, :], ALU.mult, [rpu, R("bz")], [r_yT[e]])
            ring_rel(2)
        if it == 0:
            P.dump("yT1", yT[:, 0:4, :], [128, 4, TT], BF16, rd=r_yT[0:4])

    def store_out(it):
        P.transfer(head_res() + l1_res(), [r_ost])
        for c in range(NCH):
            for hb in range(2):
                pt, rp = next_main()
                for j in range(4):
                    dc = hb * 4 + j
                    tr(pt[:, j * 128:(j + 1) * 128], xres[:, dc, c * 128:(c + 1) * 128], ident_f[:, :], [r_xres[dc], r_id], [rp], inc=(j == 3))
                evac_copy(ostage[:, c, hb * 512:(hb + 1) * 512], pt[:, :], [rp], [r_ost])
        dst = out_d[it * TT:(it + 1) * TT, :].rearrange("(c p) d -> p c d", p=128)
        P.dma(SP, dst, ostage[:, :, :], s_o, reads=[r_ost])
        P.transfer([r_ost], head_res())

    onesrow = ones_f[0:4, 0:1].to_broadcast([4, TT])
    load_x(0)
    for it in range(nt):
        cur["it"] = it
        if it % TPS == 0:
            reset_state(first=(it == 0))
        x_transposes()
        if it == 0:
            P.dump("xT", xT[:, :, :], [128, NDC, TT], BF16, rd=r_xT)
        issue_p(0, it)
        issue_p(1, it)
        gates_a()
        build_diag(0)
        head_A(0)
        for h in range(H):
            head_A1b(h)
            if h == 0:
                gates_b()
            head_A2(h)
            if h + 1 < H:
                build_diag(h + 1)
            head_BC(h)
            if h + 1 < H:
                head_A(h + 1)
            head_BC_fin(h)
        main_set[0] = MAIN_WIDE
        tail(0, it)
        layer1(it)
        tail(1, it)
        store_out(it)
        main_set[0] = MAIN
    SP.eng.wait_ge(s_o, P.dcnt[s_o])
    if P.dcnt.get(P.s_dbg, 0):
        SP.eng.wait_ge(P.s_dbg, P.dcnt[P.s_dbg])
    print("instructions emitted:", P.n_inst, "PE", PE.cnt, "ACT", ACT.cnt, "DVE", DVE.cnt, "POOL", POOL.cnt)
    return P


def _host_layout(inputs):
    f = np.float32
    a_conv_w = np.asarray(inputs["a_conv_w"], f)[0]
    b_conv_w = np.asarray(inputs["b_conv_w"], f)[0]
    par = np.zeros((128, NPAR), f)
    par[:, O_CWA:O_CWA + 64] = a_conv_w.reshape(4, NEC, 128).transpose(2, 1, 0).reshape(128, 64)
    par[:, O_CBA:O_CBA + 16] = np.asarray(inputs["a_conv_b"], f)[0].reshape(NEC, 128).T
    par[:, O_GN:O_GN + 16] = np.asarray(inputs["a_gn_w"], f)[0].reshape(NEC, 128).T
    par[:, O_SKIP:O_SKIP + 16] = np.asarray(inputs["a_skip"], f)[0].reshape(NEC, 128).T
    par[:, O_CWB:O_CWB + 48] = b_conv_w.reshape(3, NEC, 128).transpose(2, 1, 0).reshape(128, 48)
    par[:, O_LNG:O_LNG + 16] = np.asarray(inputs["ln_g"], f).reshape(2, NDC, 128).transpose(2, 0, 1).reshape(128, 16)
    par[:, O_LNB:O_LNB + 16] = np.asarray(inputs["ln_b"], f).reshape(2, NDC, 128).transpose(2, 0, 1).reshape(128, 16)
    par[0:4, O_BI] = np.asarray(inputs["a_b_i"], f)[0]
    par[0:4, O_BF] = np.asarray(inputs["a_b_f"], f)[0]
    bd = np.zeros((128, 3, NEC, 128), f)
    pidx = np.arange(128)
    for m, nm in enumerate(("a_w_q", "a_w_k", "a_w_v")):
        w = np.asarray(inputs[nm], f)[0].reshape(NEC, 32, 4, 4)
        for o in range(4):
            bd[pidx, m, :, (pidx // 4) * 4 + o] = w[:, pidx // 4, pidx % 4, o].T
    return par, np.ascontiguousarray(bd.reshape(128, 3 * NEC * 128))


_CACHE = {}


def kernel(**inputs):
    nt = int(os.environ.get("MK_NT", NT_FULL))
    dbg = tuple(x for x in os.environ.get("MK_DEBUG", "").split(",") if x)
    key = (nt, dbg)
    if key not in _CACHE:
        _CACHE[key] = build(nt, dbg)
    P = _CACHE[key]
    f = np.float32
    x = np.asarray(inputs["x"], f)
    p = np.asarray(inputs["p"], f)
    par, bd = _host_layout(inputs)
    shared = {
        "a_w_in": np.ascontiguousarray(np.asarray(inputs["a_w_in"], f)[0]),
        "a_w_out": np.ascontiguousarray(np.asarray(inputs["a_w_out"], f)[0]),
        "b_w_in": np.ascontiguousarray(np.asarray(inputs["b_w_in"], f)[0]),
        "b_w_out": np.ascontiguousarray(np.asarray(inputs["b_w_out"], f)[0]),
        "ple_w": np.ascontiguousarray(np.asarray(inputs["ple_w"], f).reshape(2 * PLE, D)),
        "ple_gate_w": np.ascontiguousarray(np.asarray(inputs["ple_gate_w"], f).reshape(2 * D, D)),
        "bd": bd,
        "params": par,
    }
    in_maps = []
    for c in range(NCORES):
        m = dict(shared)
        m["x"] = np.ascontiguousarray(x[c * BPC:(c + 1) * BPC].reshape(TOK, D))
        m["p"] = np.ascontiguousarray(p[:, c * BPC:(c + 1) * BPC].reshape(2, TOK, PLE))
        in_maps.append(m)
    res = run_bass_kernel_spmd(P.nc, in_maps, core_ids=list(range(NCORES)))
    kernel.last_results = res
    out = np.stack([np.asarray(r["out"], f).reshape(BPC, SEQ, D) for r in res.results], 0)
    return out.reshape(NCORES * BPC, SEQ, D)
```
